# Optimizing a Trainium2 kernel written in Bass

```python
import jax, jax.numpy as jnp
from jax import lax
import numpy as np

D_MODEL = 1024
BATCH = 8
SEQ = 8192
DEPTH = 1
DEC_BATCH = 16
DEC_SEQ = 64
PAST_LEN = 1024

CHUNK = 64
PLE_DIM = 256
W_A = D_MODEL
N_BLOCKS_A = 16
BLOCK_A = W_A // N_BLOCKS_A
CONV_W = 4
LRU_C = 8.0
N_HEADS_B = 16
HEAD_DIM = 64
W_B = N_HEADS_B * HEAD_DIM
Q_BLOCK = 128
N_BRANCH = 2
D_IN_TOTAL = 2 * W_A + 4 * W_B + N_HEADS_B + N_BRANCH * D_MODEL
EPS = 1e-6

kernel_name = "hawk_fox_parallel_stream_step"


def rms_norm(x, g):
    xf = x.astype(jnp.float32)
    y = xf * lax.rsqrt(jnp.mean(xf * xf, axis=-1, keepdims=True) + EPS)
    return (y * g.astype(jnp.float32)).astype(x.dtype)


def causal_conv(u, hist, w, b):
    L = u.shape[1]
    full = jnp.concatenate([hist.astype(u.dtype), u], axis=1)
    y = b
    for tap in range(CONV_W):
        y = y + full[:, tap:tap + L, :] * w[tap]
    return y, full[:, -(CONV_W - 1):, :]


def rg_lru(u, h0, w_a, b_a, w_x, b_x, a_param):
    B, L, _ = u.shape
    ub = u.reshape(B, L, N_BLOCKS_A, BLOCK_A)
    r = jax.nn.sigmoid((jnp.einsum('blnj,njk->blnk', ub, w_a).reshape(B, L, W_A) + b_a).astype(jnp.float32))
    i = jax.nn.sigmoid((jnp.einsum('blnj,njk->blnk', ub, w_x).reshape(B, L, W_A) + b_x).astype(jnp.float32))
    log_a = -LRU_C * r * jax.nn.softplus(-a_param.astype(jnp.float32))
    a = jnp.exp(log_a)
    x_in = jnp.sqrt(-jnp.expm1(2.0 * log_a)) * (i * u.astype(jnp.float32))

    def step(h, inp):
        a_t, x_t = inp
        h = a_t * h + x_t
        return h, h

    h_last, hs = lax.scan(step, h0.astype(jnp.float32),
                          (jnp.swapaxes(a, 0, 1), jnp.swapaxes(x_in, 0, 1)))
    return jnp.swapaxes(hs, 0, 1).astype(u.dtype), h_last.astype(u.dtype)


def fox_attend(q, k, v, cq, ck, qpos, kpos):
    s = jnp.einsum('qhd,khd->hqk', q, k).astype(jnp.float32) * (HEAD_DIM ** -0.5)
    s = s + cq.astype(jnp.float32).T[:, :, None] - ck.astype(jnp.float32).T[:, None, :]
    mask = kpos[None, :] <= qpos[:, None]
    s = jnp.where(mask[None], s, -jnp.inf)
    p = jax.nn.softmax(s, axis=-1)
    return jnp.einsum('hqk,khd->qhd', p.astype(v.dtype), v)


def fox_prompt(q, k, v, logf):
    B, S, H, Dh = q.shape
    cum = jnp.cumsum(logf, axis=1)
    nq = S // Q_BLOCK
    kpos = jnp.arange(S)

    def one_block(idx):
        b = idx // nq
        q0 = (idx % nq) * Q_BLOCK
        qb = lax.dynamic_slice(q, (b, q0, 0, 0), (1, Q_BLOCK, H, Dh))[0]
        cqb = lax.dynamic_slice(cum, (b, q0, 0), (1, Q_BLOCK, H))[0]
        kb = lax.dynamic_index_in_dim(k, b, 0, keepdims=False)
        vb = lax.dynamic_index_in_dim(v, b, 0, keepdims=False)
        ckb = lax.dynamic_index_in_dim(cum, b, 0, keepdims=False)
        return fox_attend(qb, kb, vb, cqb, ckb, q0 + jnp.arange(Q_BLOCK), kpos)

    out = lax.map(one_block, jnp.arange(B * nq))
    return out.reshape(B, S, H, Dh)


def fox_sample(q, k, v, logf, k_past, v_past, lf_past):
    P = k_past.shape[1]
    L = q.shape[1]
    k_all = jnp.concatenate([k_past.astype(k.dtype), k], axis=1)
    v_all = jnp.concatenate([v_past.astype(v.dtype), v], axis=1)
    cum = jnp.cumsum(jnp.concatenate([lf_past.astype(jnp.float32), logf], axis=1), axis=1)
    qpos = P + jnp.arange(L)
    kpos = jnp.arange(P + L)
    return jax.vmap(fox_attend, in_axes=(0, 0, 0, 0, 0, None, None))(q, k_all, v_all, cum[:, P:], cum, qpos, kpos)


def layer(x, pe, conv_hist, h0, k_past, v_past, lf_past, w_in, conv_w, conv_b, w_rg_a, b_rg_a, w_rg_x, b_rg_x,
          a_param, b_f, w_branch, w_o, g_pre, g_post, w_ple_gate, w_ple_proj, g_ple):
    B, L, _ = x.shape
    xn = rms_norm(x, g_pre)
    z = xn @ w_in
    sizes = (W_A, W_A, W_B, W_B, W_B, W_B, N_HEADS_B, D_MODEL, D_MODEL)
    xa, ga, q, k, v, gb, fl, ma, mb = jnp.split(z, np.cumsum(sizes)[:-1], axis=-1)
    ua, conv_new = causal_conv(xa, conv_hist, conv_w, conv_b)
    ya, h_new = rg_lru(ua, h0, w_rg_a, b_rg_a, w_rg_x, b_rg_x, a_param)
    ya = ya * jax.nn.silu(ga)
    q = q.reshape(B, L, N_HEADS_B, HEAD_DIM)
    k = k.reshape(B, L, N_HEADS_B, HEAD_DIM)
    v = v.reshape(B, L, N_HEADS_B, HEAD_DIM)
    logf = jax.nn.log_sigmoid((fl + b_f).astype(jnp.float32))
    if k_past is None:
        yb = fox_prompt(q, k, v, logf)
    else:
        yb = fox_sample(q, k, v, logf, k_past, v_past, lf_past)
    yb = yb.reshape(B, L, W_B) * jax.nn.silu(gb)
    u = jnp.einsum('bsnw,nwd->bsnd', jnp.stack([ya, yb], axis=2), w_branch)
    gates = jax.nn.sigmoid(jnp.stack([ma, mb], axis=2))
    mix = jnp.sum(gates * u, axis=2) @ w_o
    h = x + rms_norm(mix, g_post)
    ple = jax.nn.sigmoid(h @ w_ple_gate) * (pe @ w_ple_proj)
    h = h + rms_norm(ple, g_ple)
    return h, (k, v, logf, h_new, conv_new)


def setup_inputs(seed: int = 0) -> dict:
    key = jax.random.key(seed)
    ks = jax.random.split(key, 32)
    f32 = jnp.float32
    nrm = lambda k, shape, s=1.0: (jax.random.normal(k, shape, f32) * s)
    u_a = jax.random.uniform(ks[20], (DEPTH, W_A), f32, minval=0.9, maxval=0.999)
    a0 = u_a ** (1.0 / LRU_C)
    return {
        "x_prompt": nrm(ks[0], (BATCH, SEQ, D_MODEL)),
        "x_sample": nrm(ks[1], (DEC_BATCH, DEC_SEQ, D_MODEL)),
        "p_prompt": nrm(ks[2], (DEPTH, BATCH, SEQ, PLE_DIM)),
        "p_sample": nrm(ks[3], (DEPTH, DEC_BATCH, DEC_SEQ, PLE_DIM)),
        "cache_k": nrm(ks[4], (DEPTH, DEC_BATCH, PAST_LEN, N_HEADS_B, HEAD_DIM)),
        "cache_v": nrm(ks[5], (DEPTH, DEC_BATCH, PAST_LEN, N_HEADS_B, HEAD_DIM)),
        "cache_logf": jax.nn.log_sigmoid(3.0 + nrm(ks[6], (DEPTH, DEC_BATCH, PAST_LEN, N_HEADS_B))),
        "state_h": nrm(ks[7], (DEPTH, DEC_BATCH, W_A), 0.5),
        "state_conv": nrm(ks[8], (DEPTH, DEC_BATCH, CONV_W - 1, W_A)),
        "w_in": nrm(ks[9], (DEPTH, D_MODEL, D_IN_TOTAL), D_MODEL ** -0.5),
        "conv_w": nrm(ks[10], (DEPTH, CONV_W, W_A), 0.5),
        "conv_b": nrm(ks[11], (DEPTH, W_A), 0.01),
        "w_rg_a": nrm(ks[12], (DEPTH, N_BLOCKS_A, BLOCK_A, BLOCK_A), BLOCK_A ** -0.5),
        "b_rg_a": nrm(ks[13], (DEPTH, W_A), 0.01),
        "w_rg_x": nrm(ks[14], (DEPTH, N_BLOCKS_A, BLOCK_A, BLOCK_A), BLOCK_A ** -0.5),
        "b_rg_x": nrm(ks[15], (DEPTH, W_A), 0.01),
        "a_param": jnp.log(a0) - jnp.log1p(-a0),
        "b_f": 3.0 + nrm(ks[16], (DEPTH, N_HEADS_B), 0.5),
        "w_branch": nrm(ks[17], (DEPTH, N_BRANCH, W_A, D_MODEL), W_A ** -0.5),
        "w_o": nrm(ks[18], (DEPTH, D_MODEL, D_MODEL), D_MODEL ** -0.5),
        "g_pre": 1.0 + nrm(ks[19], (DEPTH, D_MODEL), 0.01),
        "g_post": 1.0 + nrm(ks[21], (DEPTH, D_MODEL), 0.01),
        "w_ple_gate": nrm(ks[22], (DEPTH, D_MODEL, D_MODEL), D_MODEL ** -0.5),
        "w_ple_proj": nrm(ks[23], (DEPTH, PLE_DIM, D_MODEL), PLE_DIM ** -0.5),
        "g_ple": 1.0 + nrm(ks[24], (DEPTH, D_MODEL), 0.01),
    }


def reference(x_prompt, x_sample, p_prompt, p_sample, cache_k, cache_v, cache_logf, state_h, state_conv,
              w_in, conv_w, conv_b, w_rg_a, b_rg_a, w_rg_x, b_rg_x, a_param, b_f, w_branch, w_o,
              g_pre, g_post, w_ple_gate, w_ple_proj, g_ple):
    hp, hs = x_prompt, x_sample
    kp_l, vp_l, fp_l, hp_l, cp_l = [], [], [], [], []
    ks_l, vs_l, fs_l, hs_l, cs_l = [], [], [], [], []
    for i in range(DEPTH):
        prm = (w_in[i], conv_w[i], conv_b[i], w_rg_a[i], b_rg_a[i], w_rg_x[i], b_rg_x[i], a_param[i], b_f[i],
               w_branch[i], w_o[i], g_pre[i], g_post[i], w_ple_gate[i], w_ple_proj[i], g_ple[i])
        Bp = hp.shape[0]
        hp, (k1, v1, f1, h1, c1) = layer(hp, p_prompt[i], jnp.zeros((Bp, CONV_W - 1, W_A), hp.dtype),
                                         jnp.zeros((Bp, W_A), jnp.float32), None, None, None, *prm)
        hs, (k2, v2, f2, h2, c2) = layer(hs, p_sample[i], state_conv[i], state_h[i],
                                         cache_k[i], cache_v[i], cache_logf[i], *prm)
        kp_l.append(k1); vp_l.append(v1); fp_l.append(f1); hp_l.append(h1); cp_l.append(c1)
        ks_l.append(k2); vs_l.append(v2); fs_l.append(f2); hs_l.append(h2); cs_l.append(c2)
    return (hp, hs,
            jnp.stack(kp_l), jnp.stack(vp_l), jnp.stack(fp_l), jnp.stack(hp_l), jnp.stack(cp_l),
            jnp.stack(ks_l), jnp.stack(vs_l), jnp.stack(fs_l), jnp.stack(hs_l), jnp.stack(cs_l))
```

```python
import contextlib
import numpy as np
import concourse.bass as bass
import concourse.mybir as mybir
from concourse.bass_utils import run_bass_kernel_spmd

F32 = mybir.dt.float32
BF16 = mybir.dt.bfloat16
AF = mybir.ActivationFunctionType
ALU = mybir.AluOpType

D = 1024
H = 16
PLE = 256
PAST = 1024
DEC = 64
EPS = 1e-6
OFF = dict(xa=0, ga=1024, q=2048, k=3072, v=4096, gb=5120, fl=6144, ma=6160, mb=7184)
ARENA_BYTES = 204 * 1024


class Sched:
    ENG = ("sp", "act", "dve", "pool", "pe")

    def __init__(self, nc):
        self.nc = nc
        self.ops = []

    def op(self, eng, fn, r=(), w=(), dma=None):
        self.ops.append(dict(eng=eng, fn=fn, r=tuple(r), w=tuple(w), dma=dma, bar=False))

    def barrier(self):
        for e in self.ENG:
            self.ops.append(dict(eng=e, fn=None, r=(), w=(), dma=None, bar=True))

    def emit(self):
        nc = self.nc
        ops = self.ops
        n = len(ops)
        last_w = {}
        readers = {}
        deps = [None] * n
        last_eng = {}
        last_dma = {}
        for i, o in enumerate(ops):
            if o["bar"]:
                dd = [j for e, j in last_eng.items() if e != o["eng"]] + list(last_dma.values())
                deps[i] = dd
                continue
            d = {}
            for k in o["r"]:
                if k in last_w:
                    d[last_w[k]] = True
            for k in o["w"]:
                if k in last_w:
                    d.setdefault(last_w[k], False)
                for rr in readers.get(k, ()):
                    d.setdefault(rr, False)
            for k in o["r"]:
                readers.setdefault(k, []).append(i)
            for k in o["w"]:
                last_w[k] = i
                readers[k] = []
            d.pop(i, None)
            dd = []
            for j, raw in d.items():
                p = ops[j]
                if p["dma"] is None and o["dma"] is None and p["eng"] == o["eng"]:
                    if o["eng"] == "pe" or not raw:
                        continue
                dd.append(j)
            deps[i] = dd
            if o["dma"] is not None:
                last_dma[o["dma"]] = i
            else:
                last_eng[o["eng"]] = i
        need = [False] * n
        for i in range(n):
            for j in deps[i]:
                need[j] = True
        cnt = {}
        semkey = [None] * n
        semval = [0] * n
        for i, o in enumerate(ops):
            if o["bar"]:
                continue
            if o["dma"] is not None:
                k = ("dma", o["dma"])
                cnt[k] = cnt.get(k, 0) + 16
                semkey[i] = k
                semval[i] = cnt[k]
            elif need[i]:
                k = ("eng", o["eng"])
                cnt[k] = cnt.get(k, 0) + 1
                semkey[i] = k
                semval[i] = cnt[k]
        keys = sorted(set(k for k in semkey if k is not None), key=str)
        self.nsem = len(keys)
        with contextlib.ExitStack() as es:
            sems = {}
            for idx, k in enumerate(keys):
                sems[k] = es.enter_context(nc.semaphore("s%d" % idx))
            block = es.enter_context(nc.Block())
            per_eng = {e: [] for e in self.ENG}
            for i, o in enumerate(ops):
                per_eng[o["eng"]].append(i)
            final = [(k, cnt[k]) for k in keys if k[0] == "dma"]

            def run(engobj, ename):
                waited = {}
                for i in per_eng[ename]:
                    o = ops[i]
                    w = {}
                    for j in deps[i]:
                        k = semkey[j]
                        w[k] = max(w.get(k, 0), semval[j])
                    for k, v in w.items():
                        if waited.get(k, 0) >= v:
                            continue
                        engobj.wait_ge(sems[k], v)
                        waited[k] = v
                    if o["bar"]:
                        continue
                    inst = o["fn"](engobj)
                    if semkey[i] is not None:
                        inst.then_inc(sems[semkey[i]], 16 if o["dma"] is not None else 1)
                if ename == "sp":
                    for k, v in final:
                        engobj.wait_ge(sems[k], v)

            @block.sync
            def _(e):
                run(e, "sp")

            @block.scalar
            def _(e):
                run(e, "act")

            @block.vector
            def _(e):
                run(e, "dve")

            @block.gpsimd
            def _(e):
                run(e, "pool")

            @block.tensor
            def _(e):
                run(e, "pe")


class Seq:
    pass


def build(S):
    nc = bass.Bass("TRN2", target_bir_lowering=False)
    di = lambda name, shape: nc.dram_tensor(name, list(shape), F32, kind="ExternalInput").ap()
    do = lambda name, shape: nc.dram_tensor(name, list(shape), F32, kind="ExternalOutput").ap()
    ds = lambda name, shape: nc.dram_tensor(name, list(shape), BF16, kind="Internal").ap()

    xp = di("xp", (S, D)); pp = di("pp", (S, PLE))
    xs = di("xs", (2, DEC, D)); psm = di("psm", (2, DEC, PLE))
    ck = di("ck", (2, PAST, D)); cv = di("cv", (2, PAST, D)); clf = di("clf", (2, PAST, H))
    sh = di("sh", (2, D)); sc = di("sc", (2, 3, D))
    w_in = di("w_in", (D, 8208)); conv_w = di("conv_w", (4, D)); conv_b = di("conv_b", (D,))
    w_rg_a = di("w_rg_a", (16, 64, 64)); b_rg_a = di("b_rg_a", (D,))
    w_rg_x = di("w_rg_x", (16, 64, 64)); b_rg_x = di("b_rg_x", (D,))
    a_param = di("a_param", (D,)); b_f = di("b_f", (H,))
    w_branch = di("w_branch", (2, D, D)); w_o = di("w_o", (D, D))
    g_pre = di("g_pre", (D,)); g_post = di("g_post", (D,))
    w_pg = di("w_pg", (D, D)); w_pp = di("w_pp", (PLE, D)); g_ple = di("g_ple", (D,))

    yp = do("yp", (S, D)); ys = do("ys", (2, DEC, D))
    kp = do("kp", (S, D)); vp = do("vp", (S, D)); lfp = do("lfp", (S, H))
    hp = do("hp", (D,)); cp = do("cp", (3, D))
    ks = do("ks", (2, DEC, D)); vs = do("vs", (2, DEC, D)); lfs = do("lfs", (2, DEC, H))
    hs_o = do("hs", (2, D)); cs = do("cs", (2, 3, D))

    wS3 = ds("wS3", (40, 128, 1024))
    wR3 = ds("wR3", (34, 128, 1024))

    seqs = []
    q = Seq(); q.name = "p"; q.L = S; q.P = 0; q.T = 512; q.BS = 128
    q.x = xp; q.p = pp; q.y = yp; q.kout = kp; q.vout = vp; q.lfout = lfp; q.hout = hp; q.cout = cp
    q.h0 = None; q.c0 = None; q.pk = None
    seqs.append(q)
    for b in range(2):
        q = Seq(); q.name = "s%d" % b; q.L = DEC; q.P = PAST; q.T = DEC; q.BS = DEC
        q.x = xs[b]; q.p = psm[b]; q.y = ys[b]; q.kout = ks[b]; q.vout = vs[b]; q.lfout = lfs[b]
        q.hout = hs_o[b]; q.cout = cs[b]; q.h0 = sh[b]; q.c0 = sc[b]
        q.pk = ck[b]; q.pv = cv[b]; q.plf = clf[b]
        seqs.append(q)
    for q in seqs:
        q.Lk = q.P + q.L
        q.nkt = (q.Lk + 127) // 128
        q.nblk = q.P // 128 + q.L // q.BS
        q.qS = ds("qS" + q.name, (8, 128, q.L))
        q.kS = ds("kS" + q.name, (8, 128, q.Lk))
        q.vS = ds("vS" + q.name, (8, q.nkt * 128, 192))
        q.yS = ds("yS" + q.name, (8, 128, q.L))

    with contextlib.ExitStack() as es:
        arena = es.enter_context(nc.sbuf_tensor("arena", [128, ARENA_BYTES // 4], F32))
        pst = [es.enter_context(nc.psum_tensor("ps%d" % i, [128, 1024], F32)) for i in range(4)]
        s = Sched(nc)
        st = dict(off=0)

        def alloc(shape, dt):
            n = 1
            for v in shape[1:]:
                n *= v
            nb = n * (4 if dt == F32 else 2)
            nb = (nb + 31) // 32 * 32
            o = st["off"]
            assert o + nb <= ARENA_BYTES, ("arena overflow", o + nb)
            st["off"] = o + nb
            ap = arena[:, o // 4:(o + nb) // 4]
            if dt != F32:
                ap = ap.bitcast(dt)
            ap = ap[:, 0:n]
            if len(shape) == 3:
                ap = ap.rearrange("p (a b) -> p a b", a=shape[1])
            elif len(shape) == 4:
                ap = ap.rearrange("p (a b c) -> p a b c", a=shape[1], b=shape[2])
            return ap

        def bank(i):
            return pst[i // 2][:, (i % 2) * 512:(i % 2) * 512 + 512]

        def bankb(i):
            return bank(i).bitcast(BF16)

        def pair(i):
            return pst[i][:, :]

        BK = lambda i: ("bank", i)

        def dma(eng, out, in_, r, w, key, slow=False):
            if slow:
                s.op(eng, lambda e: e.dma_start(out=out, in_=in_, allow_slow_non_contiguous=True), r=r, w=w, dma=key)
            else:
                s.op(eng, lambda e: e.dma_start(out=out, in_=in_), r=r, w=w, dma=key)

        def mm(out, lhsT, rhs, start, stop, r, w):
            s.op("pe", lambda e: e.matmul(out, lhsT=lhsT, rhs=rhs, start=start, stop=stop), r=r, w=w)

        def tr(out, in_, ident, r, w):
            s.op("pe", lambda e: e.transpose(out, in_, ident), r=r, w=w)

        def act(out, in_, func, r, w, bias=None, scale=None, accum=None, eng="act"):
            kw = {}
            if bias is not None:
                kw["bias"] = bias
            if scale is not None:
                kw["scale"] = scale
            if accum is not None:
                kw["accum_out"] = accum
            s.op("act", lambda e: e.activation(out=out, in_=in_, func=func, **kw), r=r, w=w)

        def tt(out, a, b, op, r, w, eng="dve"):
            s.op(eng, lambda e: e.tensor_tensor(out=out, in0=a, in1=b, op=op), r=r, w=w)

        def ts(out, a, s1, s2, op0, op1, r, w, eng="dve"):
            if op1 is None:
                s.op(eng, lambda e: e.tensor_scalar(out=out, in0=a, scalar1=s1, scalar2=None, op0=op0), r=r, w=w)
            else:
                s.op(eng, lambda e: e.tensor_scalar(out=out, in0=a, scalar1=s1, scalar2=s2, op0=op0, op1=op1), r=r, w=w)

        def stt(out, a, sc_, b, op0, op1, r, w, eng="dve"):
            s.op(eng, lambda e: e.scalar_tensor_tensor(out=out, in0=a, scalar=sc_, in1=b, op0=op0, op1=op1), r=r, w=w)

        def cp_(eng, out, in_, r, w):
            if eng == "act":
                s.op("act", lambda e: e.activation(out=out, in_=in_, func=AF.Copy), r=r, w=w)
            else:
                s.op(eng, lambda e: e.tensor_copy(out=out, in_=in_), r=r, w=w)

        def memset(ap, val, w, eng="pool"):
            s.op(eng, lambda e: e.memset(ap, val), w=w)

        ident = alloc([128, 128], BF16)
        onesb = alloc([128, 512], BF16)
        U = alloc([128, 128], F32)
        onesf = alloc([128, 128], F32)
        gpre_bc = alloc([128, D], F32); gpost_bc = alloc([128, D], F32); gple_bc = alloc([128, D], F32)
        bf_bc = alloc([128, H], F32)
        cw = alloc([128, 8, 4], F32); cb = alloc([128, 8], F32)
        bra = alloc([128, 8], F32); brx = alloc([128, 8], F32)
        nsp8 = alloc([128, 8], F32); psp8 = alloc([128, 8], F32); apar = alloc([128, 8], F32)
        junk = alloc([128, D], BF16)
        mhalf = alloc([128, 1], F32)
        memset(onesb, 1.0, ["onesb"])
        memset(onesf, 1.0, ["onesf"])
        memset(mhalf, -0.5, ["mhalf"])
        s.op("pool", lambda e: e.affine_select(out=ident, in_=onesb[:, 0:128], pattern=[[1, 128]], compare_op=ALU.is_equal,
                                               fill=0.0, base=0, channel_multiplier=-1), r=["onesb"], w=["ident"])
        s.op("pool", lambda e: e.affine_select(out=U, in_=onesf, pattern=[[1, 128]], compare_op=ALU.is_ge,
                                               fill=0.0, base=0, channel_multiplier=-1), r=["onesf"], w=["U"])
        dma("sp", gpre_bc, g_pre.partition_broadcast(128), [], ["gpre"], "c0")
        dma("sp", gpost_bc, g_post.partition_broadcast(128), [], ["gpost"], "c1")
        dma("sp", gple_bc, g_ple.partition_broadcast(128), [], ["gple"], "c2")
        dma("sp", bf_bc, b_f.partition_broadcast(128), [], ["bfbc"], "c3")
        for r_ in range(4):
            dma("sp", cw[:, :, r_:r_ + 1], conv_w[r_].rearrange("(c p o) -> p c o", p=128, o=1), [], ["cw"], "c4", slow=True)
        for tdst, tsrc, kk in ((cb, conv_b, "cb"), (bra, b_rg_a, "bra"), (brx, b_rg_x, "brx"), (apar, a_param, "apar")):
            dma("sp", tdst.rearrange("p (c o) -> p c o", o=1), tsrc.rearrange("(c p o) -> p c o", p=128, o=1), [], [kk], ("c5", kk), slow=True)
        act(nsp8, apar, AF.Exp, ["apar"], ["nsp8"], scale=-1.0)
        act(nsp8, nsp8, AF.Ln, ["nsp8"], ["nsp8"], bias=1.0)
        ts(psp8, nsp8, 8.0, None, ALU.mult, None, ["nsp8"], ["psp8"])
        ts(nsp8, nsp8, -8.0, None, ALU.mult, None, ["nsp8", "psp8"], ["nsp8"])
        hbra = alloc([128, 8], F32); hbrx = alloc([128, 8], F32)
        hnsp8 = alloc([128, 8], F32); hpsp8 = alloc([128, 8], F32)
        ts(hbra, bra, 0.5, None, ALU.mult, None, ["bra"], ["hbra"])
        ts(hbrx, brx, 0.5, None, ALU.mult, None, ["brx"], ["hbrx"])
        ts(hnsp8, nsp8, 0.5, None, ALU.mult, None, ["nsp8"], ["hnsp8"])
        ts(hpsp8, psp8, 0.5, None, ALU.mult, None, ["psp8"], ["hpsp8"])
        const_end = st["off"]

        for q in seqs:
            q.logf = alloc([128, q.nblk, H], F32)
            q.cumT = alloc([128, q.nblk, H], F32)
            q.carry = alloc([128, q.nblk + 1, H], F32)
            memset(q.cumT, 0.0, [("cumT", q.name)])
            memset(q.carry, 0.0, [("carry", q.name)])
            memset(q.logf, 0.0, [("logf", q.name)])
        persist_end = st["off"]

        def rstd_pool(rstd, ss, rows, kss):
            ts(rstd[:rows], ss[:rows], 1.0 / D, EPS, ALU.mult, ALU.add, [kss], [kss], eng="pool")
            tt(rstd[:rows], rstd[:rows], mhalf[:rows], ALU.pow, [kss, "mhalf"], [kss], eng="pool")

        def front_gen(q, tok0, xsl, kx, ss, rstd, kss, xn, kxn, xnT, kxnT, b, tpb):
            BS = q.BS
            dma("sp", xsl[:BS], q.x[tok0:tok0 + BS, :], [], [kx], kx)
            act(junk[:BS], xsl[:BS], AF.Square, [kx], ["junk", kss], accum=ss[:BS, 0:1])
            yield
            rstd_pool(rstd[:, 0:1], ss[:, 0:1], BS, kss)
            yield
            stt(xn[:BS], xsl[:BS], rstd[:BS, 0:1], gpre_bc[:BS], ALU.mult, ALU.mult, [kx, kss, "gpre"], [kxn])
            yield
            tp = bankb(tpb)
            for kc in range(8):
                tr(tp[:, kc * BS:(kc + 1) * BS], xn[:BS, kc * 128:(kc + 1) * 128], ident[:BS, :BS],
                   [kxn, "ident"], [BK(tpb)])
            yield
            cp_("act", xnT[:, :, b * BS:(b + 1) * BS], tp[:, 0:8 * BS].rearrange("p (k t) -> p k t", k=8),
                [BK(tpb)], [kxnT])

        def front(*a):
            for _ in front_gen(*a):
                pass

        def run_interleaved(gens):
            gens = [g for g in gens if g is not None]
            while gens:
                for g in list(gens):
                    try:
                        next(g)
                    except StopIteration:
                        gens.remove(g)

        wq = alloc([128, 8, D], BF16); wk = alloc([128, 8, D], BF16); wv = alloc([128, 8, D], BF16)
        wf = alloc([128, 8, H], BF16)
        wstage = alloc([128, 8, D], F32)
        for wt, pn in ((wk, "k"), (wv, "v"), (wq, "q")):
            for kc in range(8):
                dma("sp", wstage[:, kc, :], w_in[kc * 128:(kc + 1) * 128, OFF[pn]:OFF[pn] + D], [], [("wstg", kc)], ("wstg", kc))
                cp_("dve" if kc % 2 == 0 else "act", wt[:, kc, :], wstage[:, kc, :], [("wstg", kc)], [("w1", pn, kc)])
        dma("pool", wf, w_in[:, OFF["fl"]:OFF["fl"] + H].rearrange("(kc p) n -> p kc n", p=128), [], [("w1", "fl")], ("w1", "fl"))
        for pi, pn in enumerate(("xa", "ga", "gb", "ma", "mb")):
            for c in range(8):
                src = w_in[:, OFF[pn] + c * 128:OFF[pn] + c * 128 + 128].rearrange("(kc p) n -> p kc n", p=128)
                dma("pool", wS3[pi * 8 + c].rearrange("p (kc n) -> p kc n", kc=8), src, [], [("wS3", pi, c)], "c7")
        for wi, srcw in enumerate((w_branch[0], w_branch[1], w_o, w_pg)):
            for kc in range(8):
                dma("pool", wR3[wi * 8 + kc], srcw[kc * 128:(kc + 1) * 128, :], [], [("wR3", wi * 8 + kc)], "c8")
        for kc in range(2):
            dma("pool", wR3[32 + kc], w_pp[kc * 128:(kc + 1) * 128, :], [], [("wR3", 32 + kc)], "c8")
        NX = 2
        xring = [alloc([128, D], F32) for _ in range(NX)]
        ss1 = alloc([128, 2], F32); rstd1 = alloc([128, 2], F32)
        xn1 = [alloc([128, D], BF16) for _ in range(2)]
        xnT1 = [alloc([128, 8, 512], BF16) for _ in range(2)]
        ktok = [alloc([128, D], F32) for _ in range(2)]
        vtok = [alloc([128, D], F32) for _ in range(2)]
        vaug = [alloc([128, 8, 3, 64], BF16) for _ in range(2)]
        for i in range(2):
            cp_("pool", vaug[i][:, :, 1, :], onesb[:, 0:512].rearrange("p (a b) -> p a b", a=8), ["onesb"], [("vaug", i)])
        stK = [alloc([128, 8, 512], BF16) for _ in range(2)]
        stQ = [alloc([128, 8, 512], BF16) for _ in range(2)]
        flb = alloc([128, H], F32); e1 = alloc([128, H], F32)
        ckb = alloc([128, D], BF16); cvb = alloc([128, D], BF16)
        s1_cnt = dict(blk=0, tile=0, va=0, kv=0)

        def cumsum_block(q, blk, BS):
            lf = q.logf[:BS, blk, :]
            ps = bank(4)
            mm(ps[:BS, 16:32], U[:BS, :BS], lf, True, True, ["U", ("logf", q.name)], [BK(4)])
            mm(ps[:, 32:48], onesf[:BS, :], lf, True, True, ["onesf", ("logf", q.name)], [BK(4)])
            tt(q.cumT[:BS, blk, :], ps[:BS, 16:32], q.carry[:BS, blk, :], ALU.add, [BK(4), ("carry", q.name)], [("cumT", q.name)])
            tt(q.carry[:, blk + 1, :], ps[:, 32:48], q.carry[:, blk, :], ALU.add, [BK(4), ("carry", q.name)], [("carry", q.name)])

        def vaug_store(q, srcap, rows, kpos0, srckeys):
            i = s1_cnt["va"] % 2
            s1_cnt["va"] += 1
            va = vaug[i]
            cp_("dve", va[:rows, :, 0:3:2, :], srcap[:rows].rearrange("p (h e d) -> p h e d", h=8, e=2),
                srckeys, [("vaug", i)])
            dma("sp", q.vS[:, kpos0:kpos0 + rows, :].rearrange("h t c -> t h c"),
                va[:rows].rearrange("p h e d -> p h (e d)"), [("vaug", i)], [("vS", q.name)], ("vaugst", i))

        for q in seqs:
            BS, T = q.BS, q.T
            nb = T // BS
            if q.P:
                npb = q.P // 128
                dma("sp", q.logf[:, 0:npb, :], q.plf.rearrange("(i p) h -> p i h", p=128), [], [("logf", q.name)], ("plf", q.name), slow=True)
                for blk in range(npb):
                    dma("pool", ckb, q.pk[blk * 128:(blk + 1) * 128, :], [], ["ckb"], "ckb")
                    tp = bankb(6)
                    for kc in range(8):
                        tr(tp[:, kc * 128:(kc + 1) * 128], ckb[:, kc * 128:(kc + 1) * 128], ident, ["ckb", "ident"], [BK(6)])
                    sl = s1_cnt["tile"] % 2
                    s1_cnt["tile"] += 1
                    cp_("act", stK[sl][:, :, 0:128], tp.rearrange("p (k t) -> p k t", k=8), [BK(6)], [("stk", sl)])
                    dma("sp", q.kS[:, :, blk * 128:(blk + 1) * 128].rearrange("h p t -> p h t"), stK[sl][:, :, 0:128],
                        [("stk", sl)], [("kS", q.name)], ("stkst_past", sl))
                    dma("pool", cvb, q.pv[blk * 128:(blk + 1) * 128, :], [], ["cvb"], "cvb")
                    vaug_store(q, cvb, 128, blk * 128, ["cvb"])
                    cumsum_block(q, blk, 128)
            ntile1 = q.L // T
            slots = []
            for j in range(ntile1):
                slots.append(s1_cnt["tile"] % 2)
                s1_cnt["tile"] += 1

            def s1_front(j, b):
                g = s1_cnt["blk"]
                s1_cnt["blk"] += 1
                g2 = g % 2
                sl = slots[j]
                return front_gen(q, j * T + b * BS, xring[g % NX], ("x1", g % NX), ss1[:, g2:g2 + 1], rstd1[:, g2:g2 + 1], ("ss1", g2),
                                 xn1[g2], ("xn1", g2), xnT1[sl], ("xnT1", sl, b), b, 6)

            def drain(g):
                for _ in g:
                    pass

            for b in range(nb):
                drain(s1_front(0, b))
            deferred = []
            for j in range(ntile1):
                sl = slots[j]
                xnT = xnT1[sl]
                for b in range(nb):
                    g2 = s1_cnt["kv"] % 2
                    s1_cnt["kv"] += 1
                    tok0 = j * T + b * BS
                    kxnT = ("xnT1", sl, b)
                    blk = q.P // 128 + tok0 // BS
                    xk = [kxnT]
                    lhs = lambda kc: xnT[:, kc, b * BS:(b + 1) * BS]
                    for wt, pn, dst, pr, outap, ev in ((wk, "k", ktok[g2], 0, q.kout, "act"), (wv, "v", vtok[g2], 1, q.vout, "dve")):
                        for half in range(2):
                            ps = bank(pr * 2 + half)
                            for kc in range(8):
                                mm(ps[:BS, :], lhs(kc), wt[:, kc, half * 512:(half + 1) * 512], kc == 0, kc == 7,
                                   xk + [("w1", pn, kc)], [BK(pr * 2 + half)])
                            cp_(ev, dst[:BS, half * 512:(half + 1) * 512], ps[:BS, :], [BK(pr * 2 + half)], [(pn + "tok", g2)])
                        dma("act" if pn == "k" else "sp", outap[tok0:tok0 + BS, :], dst[:BS], [(pn + "tok", g2)], [], (pn + "tokst", g2))
                    vaug_store(q, vtok[g2], BS, q.P + tok0, [("vtok", g2)])
                    ps = bank(4)
                    for kc in range(8):
                        mm(ps[:BS, 0:H], lhs(kc), wf[:, kc, :], kc == 0, kc == 7, xk + [("w1", "fl")], [BK(4)])
                    for f in deferred:
                        f()
                    deferred = []
                    tt(flb[:BS], ps[:BS, 0:H], bf_bc[:BS], ALU.add, [BK(4), "bfbc"], ["flb"])
                    act(e1[:BS], flb[:BS], AF.Exp, ["flb"], ["e1"], scale=-1.0)
                    act(e1[:BS], e1[:BS], AF.Ln, ["e1"], ["e1"], bias=1.0)
                    ts(q.logf[:BS, blk, :], e1[:BS], -1.0, None, ALU.mult, None, ["e1"], [("logf", q.name)])
                    deferred.append((lambda blk=blk: cumsum_block(q, blk, BS)))
                    if j + 1 < ntile1:
                        fg = s1_front(j + 1, b)
                        next(fg); next(fg); next(fg)
                        deferred.append((lambda fg=fg: drain(fg)))
                xkall = [("xnT1", sl, b) for b in range(nb)]
                for wt, pn, stg, scr, pos0 in ((wk, "k", stK[sl], q.kS, q.P + j * T), (wq, "q", stQ[sl], q.qS, j * T)):
                    for cc in range(8):
                        bi = 5 + 2 * (cc % 2)
                        ps = bank(bi)
                        for kc in range(8):
                            mm(ps[:, :T], wt[:, kc, cc * 128:(cc + 1) * 128], xnT[:, kc, :T], kc == 0, kc == 7,
                               xkall + [("w1", pn, kc)], [BK(bi)])
                        if pn == "q":
                            ts(stg[:, cc, :T], ps[:, :T], 0.125, None, ALU.mult, None, [BK(bi)], [("st" + pn, sl)])
                        else:
                            cp_("act", stg[:, cc, :T], ps[:, :T], [BK(bi)], [("st" + pn, sl)])
                    dma("act" if pn == "k" else "sp", scr[:, :, pos0:pos0 + T].rearrange("h p t -> p h t"), stg[:, :, :T],
                        [("st" + pn, sl)], [(pn + "S", q.name)], ("st" + pn + "st", sl))
            for f in deferred:
                f()
            deferred = []
            npb = q.P // 128
            nbl = q.L // BS
            step = 16
            for i0 in range(0, nbl, step):
                i1 = min(nbl, i0 + step)
                dma("pool", q.lfout[i0 * BS:i1 * BS, :].rearrange("(i p) h -> p i h", p=BS), q.logf[:BS, npb + i0:npb + i1, :],
                    [("logf", q.name)], [], "lfout", slow=True)

        s.barrier()
        st["off"] = persist_end
        Lmax = max(q.L for q in seqs)
        Lkmax = max(q.Lk for q in seqs)
        nktmax = max(q.nkt for q in seqs)
        masks = alloc([128, 4, 512], BF16)
        for m in range(4):
            (lambda m: s.op("pool", lambda e: e.affine_select(out=masks[:, m, :], in_=onesb, pattern=[[1, 512]],
                                                              compare_op=ALU.is_ge, fill=0.0, base=-128 * m,
                                                              channel_multiplier=-1), r=["onesb"], w=["masks"]))(m)
        kTb = [alloc([128, Lkmax], BF16) for _ in range(2)]
        qTb = [alloc([128, Lmax], BF16) for _ in range(2)]
        vab = [alloc([128, nktmax, 192], BF16) for _ in range(2)]
        ybst = alloc([128, Lmax], BF16)
        pT = [alloc([128, 512], BF16) for _ in range(4)]
        rs = [alloc([128, 512], F32) for _ in range(3)]; bcs = alloc([128, 512], F32)
        biasb = [alloc([128, 64], F32) for _ in range(2)]
        for i in range(3):
            memset(rs[i], 1.0, [("rs", i)])
        OB = (4, 5, 7)
        for i in range(2):
            memset(kTb[i][0:1, :], 1.0, [("kT0", i)])
            memset(qTb[i][0:1, :], 0.0, [("qT0", i)])
        for _ in range(24):
            mm(bank(7)[:, :], masks[:, 1, 0:128], masks[:, 2, :], True, True, ["masks"], [BK(7)])
        cnt2 = dict(hd=0, pr=0, s=0, o=0, b=0)
        LA = 2
        DEFER = 7

        def head_load(q, hp_, hh, sl):
            dma("sp", kTb[sl][1:65, :q.Lk], q.kS[hp_, 64 * hh:64 * hh + 64, :], [("kS", q.name)], [("kT", sl)], ("kT", sl))
            dma("sp", qTb[sl][1:65, :q.L], q.qS[hp_, 64 * hh:64 * hh + 64, :], [("qS", q.name)], [("qT", sl)], ("qT", sl))

        def dq_chunks(q, h, sl):
            BS, Tq = q.BS, q.T
            nbt = Tq // BS
            chunks = []
            for j in range(q.L // Tq):
                for b in range(nbt):
                    def chunk(j=j, b=b):
                        cref = q.P // 128 + ((j + 1) * Tq) // BS
                        blk = j * nbt + b
                        gblk = q.P // 128 + blk
                        mm(bank(6)[0:1, b * BS:(b + 1) * BS], q.logf[:BS, gblk, h:h + 1], U[:BS, :BS], True, True,
                           [("logf", q.name), "U"], [BK(6)])
                        ts(qTb[sl][0:1, blk * BS:(blk + 1) * BS], bank(6)[0:1, b * BS:(b + 1) * BS],
                           q.carry[0:1, gblk, h:h + 1], q.carry[0:1, cref, h:h + 1], ALU.add, ALU.subtract,
                           [BK(6), ("carry", q.name)], [("qT0", sl)])
                    chunks.append(chunk)
            return chunks

        heads = [(q, hp_, hh) for q in seqs for hp_ in range(8) for hh in range(2)]
        q_, hp0, hh0 = heads[0]
        head_load(q_, hp0, hh0, 0)
        for ch in dq_chunks(q_, 2 * hp0 + hh0, 0):
            ch()
        for hidx, (q, hp_, hh) in enumerate(heads):
            Tq = q.T
            NSUB = max(1, Tq // 128)
            W = Tq // NSUB
            sl = hidx % 2
            vsl = (hidx // 2) % 2
            kT, qT, va = kTb[sl], qTb[sl], vab[vsl]
            h = 2 * hp_ + hh
            if hh == 0:
                dma("sp", va[:, :q.nkt, :], q.vS[hp_].rearrange("(i p) c -> p i c", p=128), [("vS", q.name)], [("va", vsl)], ("va", vsl))
            nxt = heads[hidx + 1] if hidx + 1 < len(heads) else None
            nxt_chunks = []
            if nxt is not None:
                head_load(nxt[0], nxt[1], nxt[2], 1 - sl)
                nxt_chunks = dq_chunks(nxt[0], 2 * nxt[1] + nxt[2], 1 - sl)
            ev = []
            for j in range(q.L // Tq):
                q0 = j * Tq
                nk = (q.P + q0 + Tq + 127) // 128
                bsl = cnt2["b"] % 2
                cnt2["b"] += 1
                ob = OB[cnt2["o"] % 3]
                cnt2["o"] += 1
                ev.append(("pre", dict(h=h, q0=q0, nk=nk, bsl=bsl, j=j)))
                for i in range(nk):
                    sn = cnt2["s"]
                    cnt2["s"] += 1
                    ev.append(("pair", dict(hh=hh, h=h, q0=q0, nk=nk, i=i, bsl=bsl, ob=ob, sn=sn)))
                ev.append(("post", dict(hh=hh, q0=q0, ob=ob)))
            pairs = [d for k_, d in ev if k_ == "pair"]

            def emit_qk(d):
                q0, i, sn = d["q0"], d["i"], d["sn"]
                rows = min(128, q.Lk - 128 * i)
                sub0 = 0
                while q.P + q0 + (sub0 + 1) * W - 1 < 128 * i:
                    sub0 += 1
                d["rows"], d["sub0"] = rows, sub0
                c0 = sub0 * W
                sb_ = sn % 4
                mm(bank(sb_)[:rows, c0:Tq], kT[0:65, 128 * i:128 * i + rows], qT[0:65, q0 + c0:q0 + Tq], True, True,
                   [("kT", sl), ("kT0", sl), ("qT", sl), ("qT0", sl)], [BK(sb_)])

            def emit_pair(d):
                hh, q0, i, sn, nk, ob = d["hh"], d["q0"], d["i"], d["sn"], d["nk"], d["ob"]
                rows, sub0 = d["rows"], d["sub0"]
                c0 = sub0 * W
                sb_ = sn % 4
                pi = sn % 4
                sps = bank(sb_)
                bb = biasb[d["bsl"]]
                bkey = ("bias", d["bsl"])
                act(pT[pi][:rows, c0:Tq], sps[:rows, c0:Tq], AF.Exp, [BK(sb_), bkey], [("pT", pi)], bias=bb[:rows, i:i + 1])
                off = q.P + q0 + sub0 * W - 128 * i
                if off < rows - 1:
                    assert off == 0
                    cs_ = slice(c0, c0 + W)
                    tt(pT[pi][:rows, cs_], pT[pi][:rows, cs_], masks[:rows, 0, 0:W], ALU.mult,
                       [("pT", pi), "masks"], [("pT", pi)])

            def emit_pv(d):
                hh, i, sn, nk, ob = d["hh"], d["i"], d["sn"], d["nk"], d["ob"]
                rows, sub0 = d["rows"], d["sub0"]
                c0 = sub0 * W
                pi = sn % 4
                mm(bank(ob)[:, c0:Tq], va[:rows, i, 64 * hh:64 * hh + 128], pT[pi][:rows, c0:Tq], i == 0, i == nk - 1,
                   [("va", vsl), ("pT", pi)], [BK(ob)])

            def emit_pre(d):
                bb = biasb[d["bsl"]]
                cref = q.P // 128 + (d["q0"] + Tq) // q.BS
                ts(bb[:, 0:d["nk"]], q.cumT[:, 0:d["nk"], d["h"]], -1.0, q.carry[:, cref, d["h"]:d["h"] + 1],
                   ALU.mult, ALU.add, [("cumT", q.name), ("carry", q.name)], [("bias", d["bsl"])])

            def emit_post1(d):
                srow = 64 if d["hh"] == 0 else 0
                ops_ = bank(d["ob"])
                rsl = OB.index(d["ob"])
                s.op("dve", (lambda srow, ops_, rsl, Tq: lambda e: e.reciprocal(out=rs[rsl][srow:srow + 1, :Tq], in_=ops_[srow:srow + 1, :Tq]))(srow, ops_, rsl, Tq),
                     r=[BK(d["ob"])], w=[("rs", rsl)])

            def emit_post2(d, q=q, Tq=Tq):
                srow = 64 if d["hh"] == 0 else 0
                orow = 0 if d["hh"] == 0 else 64
                ops_ = bank(d["ob"])
                rsl = OB.index(d["ob"])
                q0 = d["q0"]
                bc = bank(6)
                mm(bc[:, :Tq], onesf[srow:srow + 1, :], rs[rsl][srow:srow + 1, :Tq], True, True, ["onesf", ("rs", rsl)], [BK(6)])
                cp_("act", bcs[orow:orow + 64, :Tq], bc[orow:orow + 64, :Tq], [BK(6)], ["bcs"])
                tt(ybst[orow:orow + 64, q0:q0 + Tq], ops_[orow:orow + 64, :Tq], bcs[orow:orow + 64, :Tq], ALU.mult,
                   [BK(d["ob"]), "bcs"], ["ybst"])

            nqk = 0
            npair = 0
            if hh == 0:
                pending = []
            pres = [d for k_, d in ev if k_ == "pre"]
            npre = 0
            delayed = None
            emit_pre(pres[0])
            for kind, d in ev:
                if kind == "pre":
                    npre += 1
                    if npre < len(pres):
                        emit_pre(pres[npre])
                elif kind == "pair":
                    while nqk < min(len(pairs), npair + 1 + LA):
                        emit_qk(pairs[nqk])
                        nqk += 1
                    emit_pair(d)
                    if delayed is not None:
                        emit_pv(delayed)
                    delayed = d
                    npair += 1
                    if nxt_chunks:
                        nxt_chunks.pop(0)()
                    for pd in pending:
                        pd[0] -= 1
                    while pending and pending[0][0] <= 0:
                        pending.pop(0)[1]()
                else:
                    if delayed is not None:
                        emit_pv(delayed)
                        delayed = None
                    emit_post1(d)
                    pending.append([DEFER, (lambda d=d, f=emit_post2: f(d))])
            while nxt_chunks:
                nxt_chunks.pop(0)()
            if hh == 1:
                while pending:
                    pending.pop(0)[1]()
                dma("pool", q.yS[hp_], ybst[:, :q.L], ["ybst"], [("yS", q.name)], "ybstst")

        s.barrier()
        st["off"] = const_end
        wra = alloc([128, 8, 128], BF16); wrx = alloc([128, 8, 128], BF16)
        for wt, src, kk in ((wra, w_rg_a, "wra"), (wrx, w_rg_x, "wrx")):
            memset(wt, 0.0, [kk])
            sv = src.rearrange("(c e) j k -> e j c k", e=2)
            for e_ in range(2):
                dma("pool", wt[64 * e_:64 * e_ + 64, :, 64 * e_:64 * e_ + 64], sv[e_], [], [kk], ("c6", kk))
        wb0 = alloc([128, 8, D], BF16); wb1 = alloc([128, 8, D], BF16)
        wo = alloc([128, 8, D], BF16); wg = alloc([128, 8, D], BF16); wp = alloc([128, 2, D], BF16)
        for wi, (wt, kk) in enumerate(((wb0, "wb0"), (wb1, "wb1"), (wo, "wo"), (wg, "wg"))):
            dma("sp", wt, wR3[wi * 8:(wi + 1) * 8].rearrange("k p n -> p k n"), [("wR3", wi * 8 + kc) for kc in range(8)], [kk], ("w3", kk))
        dma("sp", wp, wR3[32:34].rearrange("k p n -> p k n"), [("wR3", 32), ("wR3", 33)], ["wp"], ("w3", "wp"))
        NX3 = 2
        x3 = [alloc([128, D], F32) for _ in range(NX3)]
        ss3 = alloc([128, 4], F32); rstd3 = alloc([128, 4], F32)
        xn3 = alloc([128, D], BF16)
        xnT3 = alloc([128, 8, 512], BF16)
        yaT = alloc([128, 8, 512], BF16); gybT = alloc([128, 8, 512], BF16); mrgT = alloc([128, 8, 512], BF16)
        NW = 8
        wring = [alloc([128, 8, 128], BF16) for _ in range(NW)]
        ybt = [alloc([128, 512], BF16) for _ in range(2)]
        xab = [alloc([128, 3 + 512 + 1], F32) for _ in range(2)]
        u2 = [alloc([128, 512], F32) for _ in range(2)]
        ubf = [alloc([128, 512], BF16) for _ in range(2)]
        Z = alloc([128, 5, 512], F32)
        ZK = lambda i: ("Z", i)
        rg, ig, a_, om, hsb = (Z[:, i, :] for i in range(5))
        hresb = [alloc([128, D], F32) for _ in range(2)]
        pleb = [alloc([128, D], F32) for _ in range(2)]
        sg = alloc([128, D], F32)
        sga = alloc([128, 512], F32); sgb = alloc([128, 512], F32)
        hbfb = [alloc([128, D], BF16) for _ in range(2)]
        hTb = [alloc([128, 8, 128], BF16) for _ in range(2)]
        pbf = alloc([128, 4, PLE], BF16); peT = alloc([128, 2, 128], BF16)
        hist = alloc([128, 8, 3], F32); hstate = alloc([128, 8], F32)
        c3 = dict(w=0, x=0, y=0)
        for _ in range(24):
            mm(bank(7)[:, :], onesb[:, 0:128], onesb[:, :], True, True, ["onesb"], [BK(7)])

        def wchunk(pi, c):
            sl = c3["w"] % NW
            c3["w"] += 1
            dma("sp", wring[sl], wS3[pi * 8 + c].rearrange("p (kc n) -> p kc n", kc=8), [("wS3", pi, c)], [("wr", sl)], ("wr", sl))
            return wring[sl], ("wr", sl)

        def nle():
            pass

        for q in seqs:
            BS, T = q.BS, q.T
            nb = T // BS
            ntile = q.L // T
            if q.c0 is None:
                memset(hist, 0.0, ["hist"], eng="dve")
                memset(hstate, 0.0, ["hstate"], eng="dve")
            else:
                for r_ in range(3):
                    dma("sp", hist[:, :, r_:r_ + 1], q.c0[r_].rearrange("(c p o) -> p c o", p=128, o=1), [], ["hist"], "hist", slow=True)
                dma("sp", hstate.rearrange("p (c o) -> p c o", o=1), q.h0.rearrange("(c p o) -> p c o", p=128, o=1), [], ["hstate"], "hstate", slow=True)

            def do_front(j, b):
                xi = c3["x"] % NX3
                c3["x"] += 1
                dma("pool", pbf[:BS, b, :], q.p[j * T + b * BS:j * T + (b + 1) * BS, :], [], [("pbf", b)], ("pbf", b))
                yield from front_gen(q, j * T + b * BS, x3[xi], ("x3", xi), ss3[:, 0:1], rstd3[:, 0:1], ("ss3", 0),
                                     xn3, "xn3", xnT3, ("xnT3", b), b, 6)

            for b in range(nb):
                run_interleaved([do_front(0, b)])
            for j in range(ntile):
                xkall = [("xnT3", b) for b in range(nb)]

                def A_pe(c):
                    wxa, kxa = wchunk(0, c)
                    bi = 0 if c % 2 == 0 else 5
                    ps = bank(bi)
                    for kc in range(8):
                        mm(ps[:, :T], wxa[:, kc, :], xnT3[:, kc, :T], kc == 0, kc == 7, xkall + [kxa], [BK(bi)])

                def A_act(c):
                    k = c % 2
                    bi = 0 if c % 2 == 0 else 5
                    cp_("dve", xab[k][:, 0:3], hist[:, c, :], ["hist"], [("xab", k)])
                    cp_("act", xab[k][:, 3:3 + T], bank(bi)[:, :T], [BK(bi)], [("xab", k)])
                    cp_("dve", hist[:, c, :], xab[k][:, T:T + 3], [("xab", k)], ["hist"])

                def A_pool(c):
                    k = c % 2
                    x_, u_ = xab[k], u2[k]
                    ts(u_[:, :T], x_[:, 3:3 + T], cw[:, c, 3:4], cb[:, c:c + 1], ALU.mult, ALU.add, [("xab", k), "cw", "cb"], [("u", k)])
                    for k_ in (1, 2, 3):
                        stt(u_[:, :T], x_[:, 3 - k_:3 - k_ + T], cw[:, c, 3 - k_:4 - k_], u_[:, :T], ALU.mult, ALU.add,
                            [("xab", k), "cw", ("u", k)], [("u", k)])
                    cp_("act", ubf[k][:, :T], u_[:, :T], [("u", k)], [("ubf", k)])

                def C_pe(c):
                    ba_, bb_ = (3, 4) if c % 2 == 0 else (6, 7)
                    wga, kga = wchunk(1, c)
                    for kc in range(8):
                        mm(bank(ba_)[:, :T], wga[:, kc, :], xnT3[:, kc, :T], kc == 0, kc == 7, xkall + [kga], [BK(ba_)])
                    wgb, kgb = wchunk(2, c)
                    for kc in range(8):
                        mm(bank(bb_)[:, :T], wgb[:, kc, :], xnT3[:, kc, :T], kc == 0, kc == 7, xkall + [kgb], [BK(bb_)])

                def C_act(c):
                    ba_, bb_ = (3, 4) if c % 2 == 0 else (6, 7)
                    act(sga[:, :T], bank(ba_)[:, :T], AF.Tanh, [BK(ba_)], ["sga"], scale=0.5)
                    act(sgb[:, :T], bank(bb_)[:, :T], AF.Tanh, [BK(bb_)], ["sgb"], scale=0.5)
                    stt(sga[:, :T], sga[:, :T], 1.0, bank(ba_)[:, :T], ALU.add, ALU.mult, ["sga", BK(ba_)], ["sga"])
                    stt(sgb[:, :T], sgb[:, :T], 1.0, bank(bb_)[:, :T], ALU.add, ALU.mult, ["sgb", BK(bb_)], ["sgb"])

                def B_(c):
                    k = c % 2
                    u_ = u2[k]
                    mm(bank(1)[:, :T], wra[:, c, :], ubf[k][:, :T], True, True, ["wra", ("ubf", k)], [BK(1)])
                    mm(bank(2)[:, :T], wrx[:, c, :], ubf[k][:, :T], True, True, ["wrx", ("ubf", k)], [BK(2)])
                    act(rg[:, :T], bank(1)[:, :T], AF.Tanh, [BK(1), "hbra"], [ZK(0)], bias=hbra[:, c:c + 1], scale=0.5)
                    act(ig[:, :T], bank(2)[:, :T], AF.Tanh, [BK(2), "hbrx"], [ZK(1)], bias=hbrx[:, c:c + 1], scale=0.5)
                    act(a_[:, :T], rg[:, :T], AF.Exp, [ZK(0), "hnsp8"], [ZK(2)], scale=hnsp8[:, c:c + 1], bias=hnsp8[:, c:c + 1])
                    act(rg[:, :T], rg[:, :T], AF.Tanh, [ZK(0), "hpsp8"], [ZK(0)], scale=hpsp8[:, c:c + 1], bias=hpsp8[:, c:c + 1])
                    tt(om[:, :T], a_[:, :T], a_[:, :T], ALU.mult, [ZK(2)], [ZK(3)])
                    stt(om[:, :T], om[:, :T], 1.0, rg[:, :T], ALU.add, ALU.mult, [ZK(3), ZK(0)], [ZK(3)])
                    act(om[:, :T], om[:, :T], AF.Ln, [ZK(3)], [ZK(3)])
                    act(om[:, :T], om[:, :T], AF.Exp, [ZK(3)], [ZK(3)], scale=0.5)
                    stt(ig[:, :T], ig[:, :T], 1.0, u_[:, :T], ALU.add, ALU.mult, [ZK(1), ("u", k)], [ZK(1)])
                    stt(ig[:, :T], ig[:, :T], 0.5, om[:, :T], ALU.mult, ALU.mult, [ZK(1), ZK(3)], [ZK(1)])
                    s.op("dve", (lambda c, T: lambda e: e.tensor_tensor_scan(out=hsb[:, :T], data0=a_[:, :T], data1=ig[:, :T],
                                                                           initial=hstate[:, c:c + 1], op0=ALU.mult, op1=ALU.add))(c, T),
                         r=[ZK(2), ZK(1), "hstate"], w=[ZK(4)])
                    cp_("dve", hstate[:, c:c + 1], hsb[:, T - 1:T], [ZK(4)], ["hstate"])

                def C_dve(c):
                    stt(yaT[:, c, :T], sga[:, :T], 0.5, hsb[:, :T], ALU.mult, ALU.mult, [ZK(4), "sga"], ["yaT"])
                    yi = c3["y"] % 2
                    c3["y"] += 1
                    dma("sp", ybt[yi][:, :T], q.yS[c, :, j * T:(j + 1) * T], [("yS", q.name)], [("ybt", yi)], ("ybt", yi))
                    stt(gybT[:, c, :T], sgb[:, :T], 0.5, ybt[yi][:, :T], ALU.mult, ALU.mult, [("ybt", yi), "sgb"], ["gybT"])

                A_pe(0); A_act(0); A_pool(0)
                for c in range(8):
                    C_pe(c)
                    if c < 7:
                        A_pe(c + 1)
                    C_act(c)
                    if c < 7:
                        A_act(c + 1)
                        A_pool(c + 1)
                    B_(c)
                    C_dve(c)
                sma, smb = rg, ig
                for c in range(8):
                    wma, kma = wchunk(3, c)
                    wmb, kmb = wchunk(4, c)
                    for wt_, kw_, bi in ((wma, kma, 4), (wmb, kmb, 5)):
                        ps = bank(bi)
                        for kc in range(8):
                            mm(ps[:, :T], wt_[:, kc, :], xnT3[:, kc, :T], kc == 0, kc == 7, xkall + [kw_], [BK(bi)])
                    ba = 2 * (c % 2)
                    for wt_, src, kk, bi in ((wb0, yaT, "yaT", ba), (wb1, gybT, "gybT", ba + 1)):
                        ps = bank(bi)
                        for kc in range(8):
                            mm(ps[:, :T], wt_[:, kc, c * 128:(c + 1) * 128], src[:, kc, :T], kc == 0, kc == 7,
                               [kk, "wb0", "wb1"], [BK(bi)])
                    act(sma[:, :T], bank(4)[:, :T], AF.Sigmoid, [BK(4)], [ZK(0)])
                    act(smb[:, :T], bank(5)[:, :T], AF.Sigmoid, [BK(5)], [ZK(1)])
                    tt(sma[:, :T], bank(ba)[:, :T], sma[:, :T], ALU.mult, [BK(ba), ZK(0)], [ZK(0)])
                    tt(smb[:, :T], bank(ba + 1)[:, :T], smb[:, :T], ALU.mult, [BK(ba + 1), ZK(1)], [ZK(1)])
                    tt(mrgT[:, c, :T], sma[:, :T], smb[:, :T], ALU.add, [ZK(0), ZK(1)], [("mrgT", c)])
                mk = [("mrgT", c) for c in range(8)]

                def T1(b):
                    k = b % 2
                    hres, tmp, hbf, hT = hresb[k], pleb[k], hbfb[k], hTb[k]
                    tok0 = j * T + b * BS
                    dma("sp", hres[:BS], q.x[tok0:tok0 + BS, :], [], [("hres", k)], ("xres", k))
                    mix = pair(0)
                    for half in range(2):
                        for kc in range(8):
                            mm(mix[:BS, half * 512:(half + 1) * 512], mrgT[:, kc, b * BS:(b + 1) * BS],
                               wo[:, kc, half * 512:(half + 1) * 512], kc == 0, kc == 7, mk + ["wo"], [BK(0), BK(1)])
                    yield
                    act(junk[:BS], mix[:BS], AF.Square, [BK(0), BK(1)], ["junk", ("ss3", 1)], accum=ss3[:BS, 1:2])
                    yield
                    rstd_pool(rstd3[:, 1:2], ss3[:, 1:2], BS, ("ss3", 1))
                    yield
                    stt(tmp[:BS], mix[:BS], rstd3[:BS, 1:2], gpost_bc[:BS], ALU.mult, ALU.mult,
                        [BK(0), BK(1), ("ss3", 1), "gpost"], [("ple", k)])
                    tt(hres[:BS], hres[:BS], tmp[:BS], ALU.add, [("hres", k), ("ple", k)], [("hres", k)])
                    cp_("dve", hbf[:BS], hres[:BS], [("hres", k)], [("hbf", k)])
                    yield
                    tp = bankb(7)
                    for kc in range(8):
                        tr(tp[:, kc * BS:(kc + 1) * BS], hbf[:BS, kc * 128:(kc + 1) * 128], ident[:BS, :BS], [("hbf", k), "ident"], [BK(7)])
                    yield
                    cp_("act", hT[:, :, :BS], tp[:, 0:8 * BS].rearrange("p (k t) -> p k t", k=8), [BK(7)], [("hT", k)])

                def T2(b):
                    k = b % 2
                    hres, ple, hT = hresb[k], pleb[k], hTb[k]
                    tok0 = j * T + b * BS
                    gp = pair(1)
                    for half in range(2):
                        for kc in range(8):
                            mm(gp[:BS, half * 512:(half + 1) * 512], hT[:, kc, :BS], wg[:, kc, half * 512:(half + 1) * 512],
                               kc == 0, kc == 7, [("hT", k), "wg"], [BK(2), BK(3)])
                    tp2 = bankb(6)
                    for kc in range(2):
                        tr(tp2[:, kc * BS:(kc + 1) * BS], pbf[:BS, b, kc * 128:(kc + 1) * 128], ident[:BS, :BS], [("pbf", b), "ident"], [BK(6)])
                    yield
                    cp_("act", peT[:, :, :BS], tp2[:, 0:2 * BS].rearrange("p (k t) -> p k t", k=2), [BK(6)], ["peT"])
                    pq = pair(2)
                    for half in range(2):
                        for kc in range(2):
                            mm(pq[:BS, half * 512:(half + 1) * 512], peT[:, kc, :BS], wp[:, kc, half * 512:(half + 1) * 512],
                               kc == 0, kc == 1, ["peT", "wp"], [BK(4), BK(5)])
                    act(sg[:BS], gp[:BS], AF.Sigmoid, [BK(2), BK(3)], ["sg"])
                    yield
                    tt(ple[:BS], pq[:BS], sg[:BS], ALU.mult, [BK(4), BK(5), "sg"], [("ple", k)])
                    yield
                    act(junk[:BS], ple[:BS], AF.Square, [("ple", k)], ["junk", ("ss3", 2)], accum=ss3[:BS, 2:3])
                    yield
                    rstd_pool(rstd3[:, 2:3], ss3[:, 2:3], BS, ("ss3", 2))
                    yield
                    stt(ple[:BS], ple[:BS], rstd3[:BS, 2:3], gple_bc[:BS], ALU.mult, ALU.mult, [("ple", k), ("ss3", 2), "gple"], [("ple", k)])
                    tt(ple[:BS], ple[:BS], hres[:BS], ALU.add, [("ple", k), ("hres", k)], [("ple", k)])
                    dma("pool", q.y[tok0:tok0 + BS, :], ple[:BS], [("ple", k)], [], ("yout", k))

                run_interleaved([T1(0)])
                for b in range(nb):
                    run_interleaved([T1(b + 1) if b + 1 < nb else None, T2(b),
                                     do_front(j + 1, b) if j + 1 < ntile else None])
            for r_ in range(3):
                dma("pool", q.cout[r_].rearrange("(c p o) -> p c o", p=128, o=1), hist[:, :, r_:r_ + 1], ["hist"], [], "cout", slow=True)
            dma("pool", q.hout.rearrange("(c p o) -> p c o", p=128, o=1), hstate.rearrange("p (c o) -> p c o", o=1), ["hstate"], [], "hout", slow=True)
        print("ops", len(s.ops))
        s.emit()
        print("nsem", s.nsem)
    return nc


_CACHE = {}


def _get_nc(S):
    if S not in _CACHE:
        _CACHE[S] = build(S)
    return _CACHE[S]


def kernel(x_prompt, x_sample, p_prompt, p_sample, cache_k, cache_v, cache_logf, state_h, state_conv,
           w_in, conv_w, conv_b, w_rg_a, b_rg_a, w_rg_x, b_rg_x, a_param, b_f, w_branch, w_o,
           g_pre, g_post, w_ple_gate, w_ple_proj, g_ple, _ncores=8):
    f = lambda a: np.ascontiguousarray(np.asarray(a, dtype=np.float32))
    B, S, _ = x_prompt.shape
    n = _ncores
    nc = _get_nc(S)
    shared = dict(w_in=f(w_in[0]), conv_w=f(conv_w[0]), conv_b=f(conv_b[0]), w_rg_a=f(w_rg_a[0]), b_rg_a=f(b_rg_a[0]),
                  w_rg_x=f(w_rg_x[0]), b_rg_x=f(b_rg_x[0]), a_param=f(a_param[0]), b_f=f(b_f[0]),
                  w_branch=f(w_branch[0]), w_o=f(w_o[0]), g_pre=f(g_pre[0]), g_post=f(g_post[0]),
                  w_pg=f(w_ple_gate[0]), w_pp=f(w_ple_proj[0]), g_ple=f(g_ple[0]))
    in_maps = []
    for c in range(n):
        m = dict(shared)
        m["xp"] = f(x_prompt[c]); m["pp"] = f(p_prompt[0, c])
        m["xs"] = f(x_sample[2 * c:2 * c + 2]); m["psm"] = f(p_sample[0, 2 * c:2 * c + 2])
        m["ck"] = f(np.asarray(cache_k[0, 2 * c:2 * c + 2]).reshape(2, PAST, D))
        m["cv"] = f(np.asarray(cache_v[0, 2 * c:2 * c + 2]).reshape(2, PAST, D))
        m["clf"] = f(cache_logf[0, 2 * c:2 * c + 2])
        m["sh"] = f(state_h[0, 2 * c:2 * c + 2]); m["sc"] = f(state_conv[0, 2 * c:2 * c + 2])
        in_maps.append(m)
    res = run_bass_kernel_spmd(nc, in_maps, core_ids=list(range(n))).results
    cat = lambda k: np.stack([np.asarray(r[k]) for r in res], 0)
    cat2 = lambda k: np.concatenate([np.asarray(r[k]) for r in res], 0)
    y_p = cat("yp"); y_s = cat2("ys")
    k_p = cat("kp").reshape(1, n, S, H, 64); v_p = cat("vp").reshape(1, n, S, H, 64)
    lf_p = cat("lfp").reshape(1, n, S, H)
    h_p = cat("hp").reshape(1, n, D); c_p = cat("cp").reshape(1, n, 3, D)
    k_s = cat2("ks").reshape(1, 2 * n, DEC, H, 64); v_s = cat2("vs").reshape(1, 2 * n, DEC, H, 64)
    lf_s = cat2("lfs").reshape(1, 2 * n, DEC, H)
    h_s = cat2("hs").reshape(1, 2 * n, D); c_s = cat2("cs").reshape(1, 2 * n, 3, D)
    return (y_p, y_s, k_p, v_p, lf_p, h_p, c_p, k_s, v_s, lf_s, h_s, c_s)
```

```python
import contextlib
import numpy as np
import concourse.bass as bass
import concourse.mybir as mybir
from concourse.bass_utils import run_bass_kernel_spmd

F32 = mybir.dt.float32
BF16 = mybir.dt.bfloat16
AF = mybir.ActivationFunctionType
ALU = mybir.AluOpType

D = 1024
H = 16
PLE = 256
PAST = 1024
DEC = 64
EPS = 1e-6
OFF = dict(xa=0, ga=1024, q=2048, k=3072, v=4096, gb=5120, fl=6144, ma=6160, mb=7184)
ARENA_BYTES = 204 * 1024


class Sched:
    ENG = ("sp", "act", "dve", "pool", "pe")

    def __init__(self, nc):
        self.nc = nc
        self.ops = []

    def op(self, eng, fn, r=(), w=(), dma=None):
        self.ops.append(dict(eng=eng, fn=fn, r=tuple(r), w=tuple(w), dma=dma, bar=False))

    def barrier(self):
        for e in self.ENG:
            self.ops.append(dict(eng=e, fn=None, r=(), w=(), dma=None, bar=True))

    def emit(self):
        nc = self.nc
        ops = self.ops
        n = len(ops)
        last_w = {}
        readers = {}
        deps = [None] * n
        last_eng = {}
        last_dma = {}
        for i, o in enumerate(ops):
            if o["bar"]:
                dd = [j for e, j in last_eng.items() if e != o["eng"]] + list(last_dma.values())
                deps[i] = dd
                continue
            d = {}
            for k in o["r"]:
                if k in last_w:
                    d[last_w[k]] = True
            for k in o["w"]:
                if k in last_w:
                    d.setdefault(last_w[k], False)
                for rr in readers.get(k, ()):
                    d.setdefault(rr, False)
            for k in o["r"]:
                readers.setdefault(k, []).append(i)
            for k in o["w"]:
                last_w[k] = i
                readers[k] = []
            d.pop(i, None)
            dd = []
            for j, raw in d.items():
                p = ops[j]
                if p["dma"] is None and o["dma"] is None and p["eng"] == o["eng"]:
                    if o["eng"] == "pe" or not raw:
                        continue
                dd.append(j)
            deps[i] = dd
            if o["dma"] is not None:
                last_dma[o["dma"]] = i
            else:
                last_eng[o["eng"]] = i
        need = [False] * n
        for i in range(n):
            for j in deps[i]:
                need[j] = True
        cnt = {}
        semkey = [None] * n
        semval = [0] * n
        for i, o in enumerate(ops):
            if o["bar"]:
                continue
            if o["dma"] is not None:
                k = ("dma", o["dma"])
                cnt[k] = cnt.get(k, 0) + 16
                semkey[i] = k
                semval[i] = cnt[k]
            elif need[i]:
                k = ("eng", o["eng"])
                cnt[k] = cnt.get(k, 0) + 1
                semkey[i] = k
                semval[i] = cnt[k]
        keys = sorted(set(k for k in semkey if k is not None), key=str)
        self.nsem = len(keys)
        with contextlib.ExitStack() as es:
            sems = {}
            for idx, k in enumerate(keys):
                sems[k] = es.enter_context(nc.semaphore("s%d" % idx))
            block = es.enter_context(nc.Block())
            per_eng = {e: [] for e in self.ENG}
            for i, o in enumerate(ops):
                per_eng[o["eng"]].append(i)
            final = [(k, cnt[k]) for k in keys if k[0] == "dma"]

            def run(engobj, ename):
                waited = {}
                for i in per_eng[ename]:
                    o = ops[i]
                    w = {}
                    for j in deps[i]:
                        k = semkey[j]
                        w[k] = max(w.get(k, 0), semval[j])
                    for k, v in w.items():
                        if waited.get(k, 0) >= v:
                            continue
                        engobj.wait_ge(sems[k], v)
                        waited[k] = v
                    if o["bar"]:
                        continue
                    inst = o["fn"](engobj)
                    if semkey[i] is not None:
                        inst.then_inc(sems[semkey[i]], 16 if o["dma"] is not None else 1)
                if ename == "sp":
                    for k, v in final:
                        engobj.wait_ge(sems[k], v)

            @block.sync
            def _(e):
                run(e, "sp")

            @block.scalar
            def _(e):
                run(e, "act")

            @block.vector
            def _(e):
                run(e, "dve")

            @block.gpsimd
            def _(e):
                run(e, "pool")

            @block.tensor
            def _(e):
                run(e, "pe")


class Seq:
    pass


def build(S):
    nc = bass.Bass("TRN2", target_bir_lowering=False)
    di = lambda name, shape: nc.dram_tensor(name, list(shape), F32, kind="ExternalInput").ap()
    do = lambda name, shape: nc.dram_tensor(name, list(shape), F32, kind="ExternalOutput").ap()
    ds = lambda name, shape: nc.dram_tensor(name, list(shape), BF16, kind="Internal").ap()

    xp = di("xp", (S, D)); pp = di("pp", (S, PLE))
    xs = di("xs", (2, DEC, D)); psm = di("psm", (2, DEC, PLE))
    ck = di("ck", (2, PAST, D)); cv = di("cv", (2, PAST, D)); clf = di("clf", (2, PAST, H))
    sh = di("sh", (2, D)); sc = di("sc", (2, 3, D))
    w_in = di("w_in", (D, 8208)); conv_w = di("conv_w", (4, D)); conv_b = di("conv_b", (D,))
    w_rg_a = di("w_rg_a", (16, 64, 64)); b_rg_a = di("b_rg_a", (D,))
    w_rg_x = di("w_rg_x", (16, 64, 64)); b_rg_x = di("b_rg_x", (D,))
    a_param = di("a_param", (D,)); b_f = di("b_f", (H,))
    w_branch = di("w_branch", (2, D, D)); w_o = di("w_o", (D, D))
    g_pre = di("g_pre", (D,)); g_post = di("g_post", (D,))
    w_pg = di("w_pg", (D, D)); w_pp = di("w_pp", (PLE, D)); g_ple = di("g_ple", (D,))

    yp = do("yp", (S, D)); ys = do("ys", (2, DEC, D))
    kp = do("kp", (S, D)); vp = do("vp", (S, D)); lfp = do("lfp", (S, H))
    hp = do("hp", (D,)); cp = do("cp", (3, D))
    ks = do("ks", (2, DEC, D)); vs = do("vs", (2, DEC, D)); lfs = do("lfs", (2, DEC, H))
    hs_o = do("hs", (2, D)); cs = do("cs", (2, 3, D))

    wS3 = ds("wS3", (40, 128, 1024))
    wR3 = ds("wR3", (34, 128, 1024))

    seqs = []
    q = Seq(); q.name = "p"; q.L = S; q.P = 0; q.T = 512; q.BS = 128
    q.x = xp; q.p = pp; q.y = yp; q.kout = kp; q.vout = vp; q.lfout = lfp; q.hout = hp; q.cout = cp
    q.h0 = None; q.c0 = None; q.pk = None
    seqs.append(q)
    for b in range(2):
        q = Seq(); q.name = "s%d" % b; q.L = DEC; q.P = PAST; q.T = DEC; q.BS = DEC
        q.x = xs[b]; q.p = psm[b]; q.y = ys[b]; q.kout = ks[b]; q.vout = vs[b]; q.lfout = lfs[b]
        q.hout = hs_o[b]; q.cout = cs[b]; q.h0 = sh[b]; q.c0 = sc[b]
        q.pk = ck[b]; q.pv = cv[b]; q.plf = clf[b]
        seqs.append(q)
    for q in seqs:
        q.Lk = q.P + q.L
        q.nkt = (q.Lk + 127) // 128
        q.nblk = q.P // 128 + q.L // q.BS
        q.qS = ds("qS" + q.name, (8, 128, q.L))
        q.kS = ds("kS" + q.name, (8, 128, q.Lk))
        q.vS = ds("vS" + q.name, (8, q.nkt * 128, 192))
        q.yS = ds("yS" + q.name, (8, 128, q.L))

    with contextlib.ExitStack() as es:
        arena = es.enter_context(nc.sbuf_tensor("arena", [128, ARENA_BYTES // 4], F32))
        pst = [es.enter_context(nc.psum_tensor("ps%d" % i, [128, 1024], F32)) for i in range(4)]
        s = Sched(nc)
        st = dict(off=0)

        def alloc(shape, dt):
            n = 1
            for v in shape[1:]:
                n *= v
            nb = n * (4 if dt == F32 else 2)
            nb = (nb + 31) // 32 * 32
            o = st["off"]
            assert o + nb <= ARENA_BYTES, ("arena overflow", o + nb)
            st["off"] = o + nb
            ap = arena[:, o // 4:(o + nb) // 4]
            if dt != F32:
                ap = ap.bitcast(dt)
            ap = ap[:, 0:n]
            if len(shape) == 3:
                ap = ap.rearrange("p (a b) -> p a b", a=shape[1])
            elif len(shape) == 4:
                ap = ap.rearrange("p (a b c) -> p a b c", a=shape[1], b=shape[2])
            return ap

        def bank(i):
            return pst[i // 2][:, (i % 2) * 512:(i % 2) * 512 + 512]

        def bankb(i):
            return bank(i).bitcast(BF16)

        def pair(i):
            return pst[i][:, :]

        BK = lambda i: ("bank", i)

        def dma(eng, out, in_, r, w, key, slow=False):
            if slow:
                s.op(eng, lambda e: e.dma_start(out=out, in_=in_, allow_slow_non_contiguous=True), r=r, w=w, dma=key)
            else:
                s.op(eng, lambda e: e.dma_start(out=out, in_=in_), r=r, w=w, dma=key)

        def mm(out, lhsT, rhs, start, stop, r, w):
            s.op("pe", lambda e: e.matmul(out, lhsT=lhsT, rhs=rhs, start=start, stop=stop), r=r, w=w)

        def tr(out, in_, ident, r, w):
            s.op("pe", lambda e: e.transpose(out, in_, ident), r=r, w=w)

        def act(out, in_, func, r, w, bias=None, scale=None, accum=None, eng="act"):
            kw = {}
            if bias is not None:
                kw["bias"] = bias
            if scale is not None:
                kw["scale"] = scale
            if accum is not None:
                kw["accum_out"] = accum
            s.op("act", lambda e: e.activation(out=out, in_=in_, func=func, **kw), r=r, w=w)

        def tt(out, a, b, op, r, w, eng="dve"):
            s.op(eng, lambda e: e.tensor_tensor(out=out, in0=a, in1=b, op=op), r=r, w=w)

        def ts(out, a, s1, s2, op0, op1, r, w, eng="dve"):
            if op1 is None:
                s.op(eng, lambda e: e.tensor_scalar(out=out, in0=a, scalar1=s1, scalar2=None, op0=op0), r=r, w=w)
            else:
                s.op(eng, lambda e: e.tensor_scalar(out=out, in0=a, scalar1=s1, scalar2=s2, op0=op0, op1=op1), r=r, w=w)

        def stt(out, a, sc_, b, op0, op1, r, w, eng="dve"):
            s.op(eng, lambda e: e.scalar_tensor_tensor(out=out, in0=a, scalar=sc_, in1=b, op0=op0, op1=op1), r=r, w=w)

        def cp_(eng, out, in_, r, w):
            if eng == "act":
                s.op("act", lambda e: e.activation(out=out, in_=in_, func=AF.Copy), r=r, w=w)
            else:
                s.op(eng, lambda e: e.tensor_copy(out=out, in_=in_), r=r, w=w)

        def memset(ap, val, w, eng="pool"):
            s.op(eng, lambda e: e.memset(ap, val), w=w)

        ident = alloc([128, 128], BF16)
        onesb = alloc([128, 512], BF16)
        U = alloc([128, 128], F32)
        onesf = alloc([128, 128], F32)
        gpre_bc = alloc([128, D], F32); gpost_bc = alloc([128, D], F32); gple_bc = alloc([128, D], F32)
        bf_bc = alloc([128, H], F32)
        cw = alloc([128, 8, 4], F32); cb = alloc([128, 8], F32)
        bra = alloc([128, 8], F32); brx = alloc([128, 8], F32)
        nsp8 = alloc([128, 8], F32); psp8 = alloc([128, 8], F32); apar = alloc([128, 8], F32)
        junk = alloc([128, D], BF16)
        mhalf = alloc([128, 1], F32)
        memset(onesb, 1.0, ["onesb"])
        memset(onesf, 1.0, ["onesf"])
        memset(mhalf, -0.5, ["mhalf"])
        s.op("pool", lambda e: e.affine_select(out=ident, in_=onesb[:, 0:128], pattern=[[1, 128]], compare_op=ALU.is_equal,
                                               fill=0.0, base=0, channel_multiplier=-1), r=["onesb"], w=["ident"])
        s.op("pool", lambda e: e.affine_select(out=U, in_=onesf, pattern=[[1, 128]], compare_op=ALU.is_ge,
                                               fill=0.0, base=0, channel_multiplier=-1), r=["onesf"], w=["U"])
        dma("sp", gpre_bc, g_pre.partition_broadcast(128), [], ["gpre"], "c0")
        dma("sp", gpost_bc, g_post.partition_broadcast(128), [], ["gpost"], "c1")
        dma("sp", gple_bc, g_ple.partition_broadcast(128), [], ["gple"], "c2")
        dma("sp", bf_bc, b_f.partition_broadcast(128), [], ["bfbc"], "c3")
        for r_ in range(4):
            dma("sp", cw[:, :, r_:r_ + 1], conv_w[r_].rearrange("(c p o) -> p c o", p=128, o=1), [], ["cw"], "c4", slow=True)
        for tdst, tsrc, kk in ((cb, conv_b, "cb"), (bra, b_rg_a, "bra"), (brx, b_rg_x, "brx"), (apar, a_param, "apar")):
            dma("sp", tdst.rearrange("p (c o) -> p c o", o=1), tsrc.rearrange("(c p o) -> p c o", p=128, o=1), [], [kk], ("c5", kk), slow=True)
        act(nsp8, apar, AF.Exp, ["apar"], ["nsp8"], scale=-1.0)
        act(nsp8, nsp8, AF.Ln, ["nsp8"], ["nsp8"], bias=1.0)
        ts(psp8, nsp8, 8.0, None, ALU.mult, None, ["nsp8"], ["psp8"])
        ts(nsp8, nsp8, -8.0, None, ALU.mult, None, ["nsp8", "psp8"], ["nsp8"])
        hbra = alloc([128, 8], F32); hbrx = alloc([128, 8], F32)
        hnsp8 = alloc([128, 8], F32); hpsp8 = alloc([128, 8], F32)
        ts(hbra, bra, 0.5, None, ALU.mult, None, ["bra"], ["hbra"])
        ts(hbrx, brx, 0.5, None, ALU.mult, None, ["brx"], ["hbrx"])
        ts(hnsp8, nsp8, 0.5, None, ALU.mult, None, ["nsp8"], ["hnsp8"])
        ts(hpsp8, psp8, 0.5, None, ALU.mult, None, ["psp8"], ["hpsp8"])
        const_end = st["off"]

        for q in seqs:
            q.logf = alloc([128, q.nblk, H], F32)
            q.cumT = alloc([128, q.nblk, H], F32)
            q.carry = alloc([128, q.nblk + 1, H], F32)
            memset(q.cumT, 0.0, [("cumT", q.name)])
            memset(q.carry, 0.0, [("carry", q.name)])
            memset(q.logf, 0.0, [("logf", q.name)])
        persist_end = st["off"]

        def rstd_pool(rstd, ss, rows, kss):
            ts(rstd[:rows], ss[:rows], 1.0 / D, EPS, ALU.mult, ALU.add, [kss], [kss], eng="pool")
            tt(rstd[:rows], rstd[:rows], mhalf[:rows], ALU.pow, [kss, "mhalf"], [kss], eng="pool")

        def front_gen(q, tok0, xsl, kx, ss, rstd, kss, xn, kxn, xnT, kxnT, b, tpb):
            BS = q.BS
            dma("sp", xsl[:BS], q.x[tok0:tok0 + BS, :], [], [kx], kx)
            act(junk[:BS], xsl[:BS], AF.Square, [kx], ["junk", kss], accum=ss[:BS, 0:1])
            yield
            rstd_pool(rstd[:, 0:1], ss[:, 0:1], BS, kss)
            yield
            stt(xn[:BS], xsl[:BS], rstd[:BS, 0:1], gpre_bc[:BS], ALU.mult, ALU.mult, [kx, kss, "gpre"], [kxn])
            yield
            tp = bankb(tpb)
            for kc in range(8):
                tr(tp[:, kc * BS:(kc + 1) * BS], xn[:BS, kc * 128:(kc + 1) * 128], ident[:BS, :BS],
                   [kxn, "ident"], [BK(tpb)])
            yield
            cp_("act", xnT[:, :, b * BS:(b + 1) * BS], tp[:, 0:8 * BS].rearrange("p (k t) -> p k t", k=8),
                [BK(tpb)], [kxnT])

        def front(*a):
            for _ in front_gen(*a):
                pass

        def run_interleaved(gens):
            gens = [g for g in gens if g is not None]
            while gens:
                for g in list(gens):
                    try:
                        next(g)
                    except StopIteration:
                        gens.remove(g)

        wq = alloc([128, 8, D], BF16); wk = alloc([128, 8, D], BF16); wv = alloc([128, 8, D], BF16)
        wf = alloc([128, 8, H], BF16)
        wstage = alloc([128, 8, D], F32)
        for wt, pn in ((wk, "k"), (wv, "v"), (wq, "q")):
            for kc in range(8):
                dma("sp", wstage[:, kc, :], w_in[kc * 128:(kc + 1) * 128, OFF[pn]:OFF[pn] + D], [], [("wstg", kc)], ("wstg", kc))
                cp_("dve" if kc % 2 == 0 else "act", wt[:, kc, :], wstage[:, kc, :], [("wstg", kc)], [("w1", pn, kc)])
        dma("pool", wf, w_in[:, OFF["fl"]:OFF["fl"] + H].rearrange("(kc p) n -> p kc n", p=128), [], [("w1", "fl")], ("w1", "fl"))
        for pi, pn in enumerate(("xa", "ga", "gb", "ma", "mb")):
            for c in range(8):
                src = w_in[:, OFF[pn] + c * 128:OFF[pn] + c * 128 + 128].rearrange("(kc p) n -> p kc n", p=128)
                dma("pool", wS3[pi * 8 + c].rearrange("p (kc n) -> p kc n", kc=8), src, [], [("wS3", pi, c)], "c7")
        for wi, srcw in enumerate((w_branch[0], w_branch[1], w_o, w_pg)):
            for kc in range(8):
                dma("pool", wR3[wi * 8 + kc], srcw[kc * 128:(kc + 1) * 128, :], [], [("wR3", wi * 8 + kc)], "c8")
        for kc in range(2):
            dma("pool", wR3[32 + kc], w_pp[kc * 128:(kc + 1) * 128, :], [], [("wR3", 32 + kc)], "c8")
        NX = 2
        xring = [alloc([128, D], F32) for _ in range(NX)]
        ss1 = alloc([128, 2], F32); rstd1 = alloc([128, 2], F32)
        xn1 = [alloc([128, D], BF16) for _ in range(2)]
        xnT1 = [alloc([128, 8, 512], BF16) for _ in range(2)]
        ktok = [alloc([128, D], F32) for _ in range(2)]
        vtok = [alloc([128, D], F32) for _ in range(2)]
        vaug = [alloc([128, 8, 3, 64], BF16) for _ in range(2)]
        for i in range(2):
            cp_("pool", vaug[i][:, :, 1, :], onesb[:, 0:512].rearrange("p (a b) -> p a b", a=8), ["onesb"], [("vaug", i)])
        stK = [alloc([128, 8, 512], BF16) for _ in range(2)]
        stQ = [alloc([128, 8, 512], BF16) for _ in range(2)]
        flb = alloc([128, H], F32); e1 = alloc([128, H], F32)
        ckb = alloc([128, D], BF16); cvb = alloc([128, D], BF16)
        s1_cnt = dict(blk=0, tile=0, va=0, kv=0)

        def cumsum_block(q, blk, BS):
            lf = q.logf[:BS, blk, :]
            ps = bank(4)
            mm(ps[:BS, 16:32], U[:BS, :BS], lf, True, True, ["U", ("logf", q.name)], [BK(4)])
            mm(ps[:, 32:48], onesf[:BS, :], lf, True, True, ["onesf", ("logf", q.name)], [BK(4)])
            tt(q.cumT[:BS, blk, :], ps[:BS, 16:32], q.carry[:BS, blk, :], ALU.add, [BK(4), ("carry", q.name)], [("cumT", q.name)])
            tt(q.carry[:, blk + 1, :], ps[:, 32:48], q.carry[:, blk, :], ALU.add, [BK(4), ("carry", q.name)], [("carry", q.name)])

        def vaug_store(q, srcap, rows, kpos0, srckeys):
            i = s1_cnt["va"] % 2
            s1_cnt["va"] += 1
            va = vaug[i]
            cp_("dve", va[:rows, :, 0:3:2, :], srcap[:rows].rearrange("p (h e d) -> p h e d", h=8, e=2),
                srckeys, [("vaug", i)])
            dma("sp", q.vS[:, kpos0:kpos0 + rows, :].rearrange("h t c -> t h c"),
                va[:rows].rearrange("p h e d -> p h (e d)"), [("vaug", i)], [("vS", q.name)], ("vaugst", i))

        for q in seqs:
            BS, T = q.BS, q.T
            nb = T // BS
            if q.P:
                npb = q.P // 128
                dma("sp", q.logf[:, 0:npb, :], q.plf.rearrange("(i p) h -> p i h", p=128), [], [("logf", q.name)], ("plf", q.name), slow=True)
                for blk in range(npb):
                    dma("pool", ckb, q.pk[blk * 128:(blk + 1) * 128, :], [], ["ckb"], "ckb")
                    tp = bankb(6)
                    for kc in range(8):
                        tr(tp[:, kc * 128:(kc + 1) * 128], ckb[:, kc * 128:(kc + 1) * 128], ident, ["ckb", "ident"], [BK(6)])
                    sl = s1_cnt["tile"] % 2
                    s1_cnt["tile"] += 1
                    cp_("act", stK[sl][:, :, 0:128], tp.rearrange("p (k t) -> p k t", k=8), [BK(6)], [("stk", sl)])
                    dma("sp", q.kS[:, :, blk * 128:(blk + 1) * 128].rearrange("h p t -> p h t"), stK[sl][:, :, 0:128],
                        [("stk", sl)], [("kS", q.name)], ("stkst_past", sl))
                    dma("pool", cvb, q.pv[blk * 128:(blk + 1) * 128, :], [], ["cvb"], "cvb")
                    vaug_store(q, cvb, 128, blk * 128, ["cvb"])
                    cumsum_block(q, blk, 128)
            ntile1 = q.L // T
            slots = []
            for j in range(ntile1):
                slots.append(s1_cnt["tile"] % 2)
                s1_cnt["tile"] += 1

            def s1_front(j, b):
                g = s1_cnt["blk"]
                s1_cnt["blk"] += 1
                g2 = g % 2
                sl = slots[j]
                return front_gen(q, j * T + b * BS, xring[g % NX], ("x1", g % NX), ss1[:, g2:g2 + 1], rstd1[:, g2:g2 + 1], ("ss1", g2),
                                 xn1[g2], ("xn1", g2), xnT1[sl], ("xnT1", sl, b), b, 6)

            def drain(g):
                for _ in g:
                    pass

            for b in range(nb):
                drain(s1_front(0, b))
            deferred = []
            for j in range(ntile1):
                sl = slots[j]
                xnT = xnT1[sl]
                for b in range(nb):
                    g2 = s1_cnt["kv"] % 2
                    s1_cnt["kv"] += 1
                    tok0 = j * T + b * BS
                    kxnT = ("xnT1", sl, b)
                    blk = q.P // 128 + tok0 // BS
                    xk = [kxnT]
                    lhs = lambda kc: xnT[:, kc, b * BS:(b + 1) * BS]
                    for wt, pn, dst, pr, outap, ev in ((wk, "k", ktok[g2], 0, q.kout, "act"), (wv, "v", vtok[g2], 1, q.vout, "dve")):
                        for half in range(2):
                            ps = bank(pr * 2 + half)
                            for kc in range(8):
                                mm(ps[:BS, :], lhs(kc), wt[:, kc, half * 512:(half + 1) * 512], kc == 0, kc == 7,
                                   xk + [("w1", pn, kc)], [BK(pr * 2 + half)])
                            cp_(ev, dst[:BS, half * 512:(half + 1) * 512], ps[:BS, :], [BK(pr * 2 + half)], [(pn + "tok", g2)])
                        dma("act" if pn == "k" else "sp", outap[tok0:tok0 + BS, :], dst[:BS], [(pn + "tok", g2)], [], (pn + "tokst", g2))
                    vaug_store(q, vtok[g2], BS, q.P + tok0, [("vtok", g2)])
                    ps = bank(4)
                    for kc in range(8):
                        mm(ps[:BS, 0:H], lhs(kc), wf[:, kc, :], kc == 0, kc == 7, xk + [("w1", "fl")], [BK(4)])
                    for f in deferred:
                        f()
                    deferred = []
                    tt(flb[:BS], ps[:BS, 0:H], bf_bc[:BS], ALU.add, [BK(4), "bfbc"], ["flb"])
                    act(e1[:BS], flb[:BS], AF.Exp, ["flb"], ["e1"], scale=-1.0)
                    act(e1[:BS], e1[:BS], AF.Ln, ["e1"], ["e1"], bias=1.0)
                    ts(q.logf[:BS, blk, :], e1[:BS], -1.0, None, ALU.mult, None, ["e1"], [("logf", q.name)])
                    deferred.append((lambda blk=blk: cumsum_block(q, blk, BS)))
                    if j + 1 < ntile1:
                        fg = s1_front(j + 1, b)
                        next(fg); next(fg); next(fg)
                        deferred.append((lambda fg=fg: drain(fg)))
                xkall = [("xnT1", sl, b) for b in range(nb)]
                for wt, pn, stg, scr, pos0 in ((wk, "k", stK[sl], q.kS, q.P + j * T), (wq, "q", stQ[sl], q.qS, j * T)):
                    for cc in range(8):
                        bi = 5 + 2 * (cc % 2)
                        ps = bank(bi)
                        for kc in range(8):
                            mm(ps[:, :T], wt[:, kc, cc * 128:(cc + 1) * 128], xnT[:, kc, :T], kc == 0, kc == 7,
                               xkall + [("w1", pn, kc)], [BK(bi)])
                        if pn == "q":
                            ts(stg[:, cc, :T], ps[:, :T], 0.125, None, ALU.mult, None, [BK(bi)], [("st" + pn, sl)])
                        else:
                            cp_("act", stg[:, cc, :T], ps[:, :T], [BK(bi)], [("st" + pn, sl)])
                    dma("act" if pn == "k" else "sp", scr[:, :, pos0:pos0 + T].rearrange("h p t -> p h t"), stg[:, :, :T],
                        [("st" + pn, sl)], [(pn + "S", q.name)], ("st" + pn + "st", sl))
            for f in deferred:
                f()
            deferred = []
            npb = q.P // 128
            nbl = q.L // BS
            step = 16
            for i0 in range(0, nbl, step):
                i1 = min(nbl, i0 + step)
                dma("pool", q.lfout[i0 * BS:i1 * BS, :].rearrange("(i p) h -> p i h", p=BS), q.logf[:BS, npb + i0:npb + i1, :],
                    [("logf", q.name)], [], "lfout", slow=True)

        s.barrier()
        st["off"] = persist_end
        Lmax = max(q.L for q in seqs)
        Lkmax = max(q.Lk for q in seqs)
        nktmax = max(q.nkt for q in seqs)
        masks = alloc([128, 4, 512], BF16)
        for m in range(4):
            (lambda m: s.op("pool", lambda e: e.affine_select(out=masks[:, m, :], in_=onesb, pattern=[[1, 512]],
                                                              compare_op=ALU.is_ge, fill=0.0, base=-128 * m,
                                                              channel_multiplier=-1), r=["onesb"], w=["masks"]))(m)
        kTb = [alloc([128, Lkmax], BF16) for _ in range(2)]
        qTb = [alloc([128, Lmax], BF16) for _ in range(2)]
        vab = [alloc([128, nktmax, 192], BF16) for _ in range(2)]
        ybst = alloc([128, Lmax], BF16)
        pT = [alloc([128, 512], BF16) for _ in range(4)]
        rs = [alloc([128, 512], F32) for _ in range(3)]; bcs = alloc([128, 512], F32)
        biasb = [alloc([128, 64], F32) for _ in range(2)]
        for i in range(3):
            memset(rs[i], 1.0, [("rs", i)])
        OB = (4, 5, 7)
        for i in range(2):
            memset(kTb[i][0:1, :], 1.0, [("kT0", i)])
            memset(qTb[i][0:1, :], 0.0, [("qT0", i)])
        for _ in range(24):
            mm(bank(7)[:, :], masks[:, 1, 0:128], masks[:, 2, :], True, True, ["masks"], [BK(7)])
        cnt2 = dict(hd=0, pr=0, s=0, o=0, b=0)
        LA = 2
        DEFER = 7

        def head_load(q, hp_, hh, sl):
            dma("sp", kTb[sl][1:65, :q.Lk], q.kS[hp_, 64 * hh:64 * hh + 64, :], [("kS", q.name)], [("kT", sl)], ("kT", sl))
            dma("sp", qTb[sl][1:65, :q.L], q.qS[hp_, 64 * hh:64 * hh + 64, :], [("qS", q.name)], [("qT", sl)], ("qT", sl))

        def dq_chunks(q, h, sl):
            BS, Tq = q.BS, q.T
            nbt = Tq // BS
            chunks = []
            for j in range(q.L // Tq):
                def chunk(j=j):
                    cref = q.P // 128 + ((j + 1) * Tq) // BS
                    for b in range(nbt):
                        blk = j * nbt + b
                        gblk = q.P // 128 + blk
                        mm(bank(6)[0:1, b * BS:(b + 1) * BS], q.logf[:BS, gblk, h:h + 1], U[:BS, :BS], True, True,
                           [("logf", q.name), "U"], [BK(6)])
                    for b in range(nbt):
                        blk = j * nbt + b
                        gblk = q.P // 128 + blk
                        ts(qTb[sl][0:1, blk * BS:(blk + 1) * BS], bank(6)[0:1, b * BS:(b + 1) * BS],
                           q.carry[0:1, gblk, h:h + 1], q.carry[0:1, cref, h:h + 1], ALU.add, ALU.subtract,
                           [BK(6), ("carry", q.name)], [("qT0", sl)])
                chunks.append(chunk)
            return chunks

        heads = [(q, hp_, hh) for q in seqs for hp_ in range(8) for hh in range(2)]
        q_, hp0, hh0 = heads[0]
        head_load(q_, hp0, hh0, 0)
        for ch in dq_chunks(q_, 2 * hp0 + hh0, 0):
            ch()
        for hidx, (q, hp_, hh) in enumerate(heads):
            Tq = q.T
            NSUB = max(1, Tq // 128)
            W = Tq // NSUB
            sl = hidx % 2
            vsl = (hidx // 2) % 2
            kT, qT, va = kTb[sl], qTb[sl], vab[vsl]
            h = 2 * hp_ + hh
            if hh == 0:
                dma("sp", va[:, :q.nkt, :], q.vS[hp_].rearrange("(i p) c -> p i c", p=128), [("vS", q.name)], [("va", vsl)], ("va", vsl))
            nxt = heads[hidx + 1] if hidx + 1 < len(heads) else None
            nxt_chunks = []
            if nxt is not None:
                head_load(nxt[0], nxt[1], nxt[2], 1 - sl)
                nxt_chunks = dq_chunks(nxt[0], 2 * nxt[1] + nxt[2], 1 - sl)
            ev = []
            for j in range(q.L // Tq):
                q0 = j * Tq
                nk = (q.P + q0 + Tq + 127) // 128
                bsl = cnt2["b"] % 2
                cnt2["b"] += 1
                ob = OB[cnt2["o"] % 3]
                cnt2["o"] += 1
                ev.append(("pre", dict(h=h, q0=q0, nk=nk, bsl=bsl, j=j)))
                for i in range(nk):
                    sn = cnt2["s"]
                    cnt2["s"] += 1
                    ev.append(("pair", dict(hh=hh, h=h, q0=q0, nk=nk, i=i, bsl=bsl, ob=ob, sn=sn)))
                ev.append(("post", dict(hh=hh, q0=q0, ob=ob)))
            pairs = [d for k_, d in ev if k_ == "pair"]

            def emit_qk(d):
                q0, i, sn = d["q0"], d["i"], d["sn"]
                rows = min(128, q.Lk - 128 * i)
                sub0 = 0
                while q.P + q0 + (sub0 + 1) * W - 1 < 128 * i:
                    sub0 += 1
                d["rows"], d["sub0"] = rows, sub0
                c0 = sub0 * W
                sb_ = sn % 4
                mm(bank(sb_)[:rows, c0:Tq], kT[0:65, 128 * i:128 * i + rows], qT[0:65, q0 + c0:q0 + Tq], True, True,
                   [("kT", sl), ("kT0", sl), ("qT", sl), ("qT0", sl)], [BK(sb_)])

            def emit_pair(d):
                hh, q0, i, sn, nk, ob = d["hh"], d["q0"], d["i"], d["sn"], d["nk"], d["ob"]
                rows, sub0 = d["rows"], d["sub0"]
                c0 = sub0 * W
                sb_ = sn % 4
                pi = sn % 4
                sps = bank(sb_)
                bb = biasb[d["bsl"]]
                bkey = ("bias", d["bsl"])
                act(pT[pi][:rows, c0:Tq], sps[:rows, c0:Tq], AF.Exp, [BK(sb_), bkey], [("pT", pi)], bias=bb[:rows, i:i + 1])
                off = q.P + q0 + sub0 * W - 128 * i
                if off < rows - 1:
                    assert off == 0
                    cs_ = slice(c0, c0 + W)
                    tt(pT[pi][:rows, cs_], pT[pi][:rows, cs_], masks[:rows, 0, 0:W], ALU.mult,
                       [("pT", pi), "masks"], [("pT", pi)])

            def emit_pv(d):
                hh, i, sn, nk, ob = d["hh"], d["i"], d["sn"], d["nk"], d["ob"]
                rows, sub0 = d["rows"], d["sub0"]
                c0 = sub0 * W
                pi = sn % 4
                mm(bank(ob)[:, c0:Tq], va[:rows, i, 64 * hh:64 * hh + 128], pT[pi][:rows, c0:Tq], i == 0, i == nk - 1,
                   [("va", vsl), ("pT", pi)], [BK(ob)])

            def emit_pre(d):
                bb = biasb[d["bsl"]]
                cref = q.P // 128 + (d["q0"] + Tq) // q.BS
                ts(bb[:, 0:d["nk"]], q.cumT[:, 0:d["nk"], d["h"]], -1.0, q.carry[:, cref, d["h"]:d["h"] + 1],
                   ALU.mult, ALU.add, [("cumT", q.name), ("carry", q.name)], [("bias", d["bsl"])])

            def emit_post1(d):
                srow = 64 if d["hh"] == 0 else 0
                ops_ = bank(d["ob"])
                rsl = OB.index(d["ob"])
                s.op("dve", (lambda srow, ops_, rsl, Tq: lambda e: e.reciprocal(out=rs[rsl][srow:srow + 1, :Tq], in_=ops_[srow:srow + 1, :Tq]))(srow, ops_, rsl, Tq),
                     r=[BK(d["ob"])], w=[("rs", rsl)])

            def emit_post2(d, q=q, Tq=Tq):
                srow = 64 if d["hh"] == 0 else 0
                orow = 0 if d["hh"] == 0 else 64
                ops_ = bank(d["ob"])
                rsl = OB.index(d["ob"])
                q0 = d["q0"]
                bc = bank(6)
                mm(bc[:, :Tq], onesf[srow:srow + 1, :], rs[rsl][srow:srow + 1, :Tq], True, True, ["onesf", ("rs", rsl)], [BK(6)])
                cp_("dve", bcs[orow:orow + 64, :Tq], bc[orow:orow + 64, :Tq], [BK(6)], ["bcs"])
                tt(ybst[orow:orow + 64, q0:q0 + Tq], ops_[orow:orow + 64, :Tq], bcs[orow:orow + 64, :Tq], ALU.mult,
                   [BK(d["ob"]), "bcs"], ["ybst"])

            nqk = 0
            npair = 0
            if hh == 0:
                pending = []
            pres = [d for k_, d in ev if k_ == "pre"]
            npre = 0
            delayed = None
            emit_pre(pres[0])
            for kind, d in ev:
                if kind == "pre":
                    npre += 1
                    if npre < len(pres):
                        emit_pre(pres[npre])
                elif kind == "pair":
                    while nqk < min(len(pairs), npair + 1 + LA):
                        emit_qk(pairs[nqk])
                        nqk += 1
                    emit_pair(d)
                    if delayed is not None:
                        emit_pv(delayed)
                    delayed = d
                    npair += 1
                    if nxt_chunks and npair % 2 == 0:
                        nxt_chunks.pop(0)()
                    for pd in pending:
                        pd[0] -= 1
                    while pending and pending[0][0] <= 0:
                        pending.pop(0)[1]()
                else:
                    if delayed is not None:
                        emit_pv(delayed)
                        delayed = None
                    emit_post1(d)
                    pending.append([DEFER, (lambda d=d, f=emit_post2: f(d))])
            while nxt_chunks:
                nxt_chunks.pop(0)()
            if hh == 1:
                while pending:
                    pending.pop(0)[1]()
                dma("pool", q.yS[hp_], ybst[:, :q.L], ["ybst"], [("yS", q.name)], "ybstst")

        s.barrier()
        st["off"] = const_end
        wra = alloc([128, 8, 128], BF16); wrx = alloc([128, 8, 128], BF16)
        for wt, src, kk in ((wra, w_rg_a, "wra"), (wrx, w_rg_x, "wrx")):
            memset(wt, 0.0, [kk])
            sv = src.rearrange("(c e) j k -> e j c k", e=2)
            for e_ in range(2):
                dma("pool", wt[64 * e_:64 * e_ + 64, :, 64 * e_:64 * e_ + 64], sv[e_], [], [kk], ("c6", kk))
        wb0 = alloc([128, 8, D], BF16); wb1 = alloc([128, 8, D], BF16)
        wo = alloc([128, 8, D], BF16); wg = alloc([128, 8, D], BF16); wp = alloc([128, 2, D], BF16)
        for wi, (wt, kk) in enumerate(((wb0, "wb0"), (wb1, "wb1"), (wo, "wo"), (wg, "wg"))):
            dma("sp", wt, wR3[wi * 8:(wi + 1) * 8].rearrange("k p n -> p k n"), [("wR3", wi * 8 + kc) for kc in range(8)], [kk], ("w3", kk))
        dma("sp", wp, wR3[32:34].rearrange("k p n -> p k n"), [("wR3", 32), ("wR3", 33)], ["wp"], ("w3", "wp"))
        NX3 = 2
        x3 = [alloc([128, D], F32) for _ in range(NX3)]
        ss3 = alloc([128, 4], F32); rstd3 = alloc([128, 4], F32)
        xn3 = alloc([128, D], BF16)
        xnT3 = alloc([128, 8, 512], BF16)
        yaT = alloc([128, 8, 512], BF16); gybT = alloc([128, 8, 512], BF16); mrgT = alloc([128, 8, 512], BF16)
        NW = 8
        wring = [alloc([128, 8, 128], BF16) for _ in range(NW)]
        ybt = [alloc([128, 512], BF16) for _ in range(2)]
        xab = [alloc([128, 3 + 512 + 1], F32) for _ in range(2)]
        u2 = [alloc([128, 512], F32) for _ in range(2)]
        ubf = [alloc([128, 512], BF16) for _ in range(2)]
        Z = alloc([128, 5, 512], F32)
        ZK = lambda i: ("Z", i)
        rg, ig, a_, om, hsb = (Z[:, i, :] for i in range(5))
        hresb = [alloc([128, D], F32) for _ in range(2)]
        pleb = [alloc([128, D], F32) for _ in range(2)]
        sg = alloc([128, D], F32)
        sga = alloc([128, 512], F32); sgb = alloc([128, 512], F32)
        hbfb = [alloc([128, D], BF16) for _ in range(2)]
        hTb = [alloc([128, 8, 128], BF16) for _ in range(2)]
        pbf = alloc([128, 4, PLE], BF16); peT = alloc([128, 2, 128], BF16)
        hist = alloc([128, 8, 3], F32); hstate = alloc([128, 8], F32)
        c3 = dict(w=0, x=0, y=0)
        for _ in range(24):
            mm(bank(7)[:, :], onesb[:, 0:128], onesb[:, :], True, True, ["onesb"], [BK(7)])

        def wchunk(pi, c):
            sl = c3["w"] % NW
            c3["w"] += 1
            dma("sp", wring[sl], wS3[pi * 8 + c].rearrange("p (kc n) -> p kc n", kc=8), [("wS3", pi, c)], [("wr", sl)], ("wr", sl))
            return wring[sl], ("wr", sl)

        def nle():
            pass

        for q in seqs:
            BS, T = q.BS, q.T
            nb = T // BS
            ntile = q.L // T
            if q.c0 is None:
                memset(hist, 0.0, ["hist"], eng="dve")
                memset(hstate, 0.0, ["hstate"], eng="dve")
            else:
                for r_ in range(3):
                    dma("sp", hist[:, :, r_:r_ + 1], q.c0[r_].rearrange("(c p o) -> p c o", p=128, o=1), [], ["hist"], "hist", slow=True)
                dma("sp", hstate.rearrange("p (c o) -> p c o", o=1), q.h0.rearrange("(c p o) -> p c o", p=128, o=1), [], ["hstate"], "hstate", slow=True)

            def do_front(j, b):
                xi = c3["x"] % NX3
                c3["x"] += 1
                dma("pool", pbf[:BS, b, :], q.p[j * T + b * BS:j * T + (b + 1) * BS, :], [], [("pbf", b)], ("pbf", b))
                yield from front_gen(q, j * T + b * BS, x3[xi], ("x3", xi), ss3[:, 0:1], rstd3[:, 0:1], ("ss3", 0),
                                     xn3, "xn3", xnT3, ("xnT3", b), b, 6)

            for b in range(nb):
                run_interleaved([do_front(0, b)])
            for j in range(ntile):
                xkall = [("xnT3", b) for b in range(nb)]

                def A_pe(c):
                    wxa, kxa = wchunk(0, c)
                    bi = 0 if c % 2 == 0 else 5
                    ps = bank(bi)
                    for kc in range(8):
                        mm(ps[:, :T], wxa[:, kc, :], xnT3[:, kc, :T], kc == 0, kc == 7, xkall + [kxa], [BK(bi)])

                def A_act(c):
                    k = c % 2
                    bi = 0 if c % 2 == 0 else 5
                    cp_("dve", xab[k][:, 0:3], hist[:, c, :], ["hist"], [("xab", k)])
                    cp_("act", xab[k][:, 3:3 + T], bank(bi)[:, :T], [BK(bi)], [("xab", k)])
                    cp_("dve", hist[:, c, :], xab[k][:, T:T + 3], [("xab", k)], ["hist"])

                def A_pool(c):
                    k = c % 2
                    x_, u_ = xab[k], u2[k]
                    ts(u_[:, :T], x_[:, 3:3 + T], cw[:, c, 3:4], cb[:, c:c + 1], ALU.mult, ALU.add, [("xab", k), "cw", "cb"], [("u", k)])
                    for k_ in (1, 2, 3):
                        stt(u_[:, :T], x_[:, 3 - k_:3 - k_ + T], cw[:, c, 3 - k_:4 - k_], u_[:, :T], ALU.mult, ALU.add,
                            [("xab", k), "cw", ("u", k)], [("u", k)])
                    cp_("act", ubf[k][:, :T], u_[:, :T], [("u", k)], [("ubf", k)])

                def C_pe(c):
                    ba_, bb_ = (3, 4) if c % 2 == 0 else (6, 7)
                    wga, kga = wchunk(1, c)
                    for kc in range(8):
                        mm(bank(ba_)[:, :T], wga[:, kc, :], xnT3[:, kc, :T], kc == 0, kc == 7, xkall + [kga], [BK(ba_)])
                    wgb, kgb = wchunk(2, c)
                    for kc in range(8):
                        mm(bank(bb_)[:, :T], wgb[:, kc, :], xnT3[:, kc, :T], kc == 0, kc == 7, xkall + [kgb], [BK(bb_)])

                def C_act(c):
                    ba_, bb_ = (3, 4) if c % 2 == 0 else (6, 7)
                    act(sga[:, :T], bank(ba_)[:, :T], AF.Tanh, [BK(ba_)], ["sga"], scale=0.5)
                    act(sgb[:, :T], bank(bb_)[:, :T], AF.Tanh, [BK(bb_)], ["sgb"], scale=0.5)
                    stt(sga[:, :T], sga[:, :T], 1.0, bank(ba_)[:, :T], ALU.add, ALU.mult, ["sga", BK(ba_)], ["sga"])
                    stt(sgb[:, :T], sgb[:, :T], 1.0, bank(bb_)[:, :T], ALU.add, ALU.mult, ["sgb", BK(bb_)], ["sgb"])

                def B_(c):
                    k = c % 2
                    u_ = u2[k]
                    mm(bank(1)[:, :T], wra[:, c, :], ubf[k][:, :T], True, True, ["wra", ("ubf", k)], [BK(1)])
                    mm(bank(2)[:, :T], wrx[:, c, :], ubf[k][:, :T], True, True, ["wrx", ("ubf", k)], [BK(2)])
                    act(rg[:, :T], bank(1)[:, :T], AF.Tanh, [BK(1), "hbra"], [ZK(0)], bias=hbra[:, c:c + 1], scale=0.5)
                    act(ig[:, :T], bank(2)[:, :T], AF.Tanh, [BK(2), "hbrx"], [ZK(1)], bias=hbrx[:, c:c + 1], scale=0.5)
                    act(a_[:, :T], rg[:, :T], AF.Exp, [ZK(0), "hnsp8"], [ZK(2)], scale=hnsp8[:, c:c + 1], bias=hnsp8[:, c:c + 1])
                    act(rg[:, :T], rg[:, :T], AF.Tanh, [ZK(0), "hpsp8"], [ZK(0)], scale=hpsp8[:, c:c + 1], bias=hpsp8[:, c:c + 1])
                    tt(om[:, :T], a_[:, :T], a_[:, :T], ALU.mult, [ZK(2)], [ZK(3)])
                    stt(om[:, :T], om[:, :T], 1.0, rg[:, :T], ALU.add, ALU.mult, [ZK(3), ZK(0)], [ZK(3)])
                    act(om[:, :T], om[:, :T], AF.Ln, [ZK(3)], [ZK(3)])
                    act(om[:, :T], om[:, :T], AF.Exp, [ZK(3)], [ZK(3)], scale=0.5)
                    stt(ig[:, :T], ig[:, :T], 1.0, u_[:, :T], ALU.add, ALU.mult, [ZK(1), ("u", k)], [ZK(1)])
                    stt(ig[:, :T], ig[:, :T], 0.5, om[:, :T], ALU.mult, ALU.mult, [ZK(1), ZK(3)], [ZK(1)])
                    s.op("dve", (lambda c, T: lambda e: e.tensor_tensor_scan(out=hsb[:, :T], data0=a_[:, :T], data1=ig[:, :T],
                                                                           initial=hstate[:, c:c + 1], op0=ALU.mult, op1=ALU.add))(c, T),
                         r=[ZK(2), ZK(1), "hstate"], w=[ZK(4)])
                    cp_("dve", hstate[:, c:c + 1], hsb[:, T - 1:T], [ZK(4)], ["hstate"])

                def C_dve(c):
                    stt(yaT[:, c, :T], sga[:, :T], 0.5, hsb[:, :T], ALU.mult, ALU.mult, [ZK(4), "sga"], ["yaT"])
                    yi = c3["y"] % 2
                    c3["y"] += 1
                    dma("sp", ybt[yi][:, :T], q.yS[c, :, j * T:(j + 1) * T], [("yS", q.name)], [("ybt", yi)], ("ybt", yi))
                    stt(gybT[:, c, :T], sgb[:, :T], 0.5, ybt[yi][:, :T], ALU.mult, ALU.mult, [("ybt", yi), "sgb"], ["gybT"])

                A_pe(0); A_act(0); A_pool(0)
                for c in range(8):
                    C_pe(c)
                    if c < 7:
                        A_pe(c + 1)
                    C_act(c)
                    if c < 7:
                        A_act(c + 1)
                        A_pool(c + 1)
                    B_(c)
                    C_dve(c)
                sma, smb = rg, ig
                for c in range(8):
                    wma, kma = wchunk(3, c)
                    wmb, kmb = wchunk(4, c)
                    for wt_, kw_, bi in ((wma, kma, 4), (wmb, kmb, 5)):
                        ps = bank(bi)
                        for kc in range(8):
                            mm(ps[:, :T], wt_[:, kc, :], xnT3[:, kc, :T], kc == 0, kc == 7, xkall + [kw_], [BK(bi)])
                    ba = 2 * (c % 2)
                    for wt_, src, kk, bi in ((wb0, yaT, "yaT", ba), (wb1, gybT, "gybT", ba + 1)):
                        ps = bank(bi)
                        for kc in range(8):
                            mm(ps[:, :T], wt_[:, kc, c * 128:(c + 1) * 128], src[:, kc, :T], kc == 0, kc == 7,
                               [kk, "wb0", "wb1"], [BK(bi)])
                    act(sma[:, :T], bank(4)[:, :T], AF.Sigmoid, [BK(4)], [ZK(0)])
                    act(smb[:, :T], bank(5)[:, :T], AF.Sigmoid, [BK(5)], [ZK(1)])
                    tt(sma[:, :T], bank(ba)[:, :T], sma[:, :T], ALU.mult, [BK(ba), ZK(0)], [ZK(0)])
                    tt(smb[:, :T], bank(ba + 1)[:, :T], smb[:, :T], ALU.mult, [BK(ba + 1), ZK(1)], [ZK(1)])
                    tt(mrgT[:, c, :T], sma[:, :T], smb[:, :T], ALU.add, [ZK(0), ZK(1)], [("mrgT", c)])
                mk = [("mrgT", c) for c in range(8)]

                def T1(b):
                    k = b % 2
                    hres, tmp, hbf, hT = hresb[k], pleb[k], hbfb[k], hTb[k]
                    tok0 = j * T + b * BS
                    dma("sp", hres[:BS], q.x[tok0:tok0 + BS, :], [], [("hres", k)], ("xres", k))
                    mix = pair(0)
                    for half in range(2):
                        for kc in range(8):
                            mm(mix[:BS, half * 512:(half + 1) * 512], mrgT[:, kc, b * BS:(b + 1) * BS],
                               wo[:, kc, half * 512:(half + 1) * 512], kc == 0, kc == 7, mk + ["wo"], [BK(0), BK(1)])
                    yield
                    act(junk[:BS], mix[:BS], AF.Square, [BK(0), BK(1)], ["junk", ("ss3", 1)], accum=ss3[:BS, 1:2])
                    yield
                    rstd_pool(rstd3[:, 1:2], ss3[:, 1:2], BS, ("ss3", 1))
                    yield
                    stt(tmp[:BS], mix[:BS], rstd3[:BS, 1:2], gpost_bc[:BS], ALU.mult, ALU.mult,
                        [BK(0), BK(1), ("ss3", 1), "gpost"], [("ple", k)])
                    tt(hres[:BS], hres[:BS], tmp[:BS], ALU.add, [("hres", k), ("ple", k)], [("hres", k)])
                    cp_("dve", hbf[:BS], hres[:BS], [("hres", k)], [("hbf", k)])
                    yield
                    tp = bankb(7)
                    for kc in range(8):
                        tr(tp[:, kc * BS:(kc + 1) * BS], hbf[:BS, kc * 128:(kc + 1) * 128], ident[:BS, :BS], [("hbf", k), "ident"], [BK(7)])
                    yield
                    cp_("act", hT[:, :, :BS], tp[:, 0:8 * BS].rearrange("p (k t) -> p k t", k=8), [BK(7)], [("hT", k)])

                def T2(b):
                    k = b % 2
                    hres, ple, hT = hresb[k], pleb[k], hTb[k]
                    tok0 = j * T + b * BS
                    gp = pair(1)
                    for half in range(2):
                        for kc in range(8):
                            mm(gp[:BS, half * 512:(half + 1) * 512], hT[:, kc, :BS], wg[:, kc, half * 512:(half + 1) * 512],
                               kc == 0, kc == 7, [("hT", k), "wg"], [BK(2), BK(3)])
                    tp2 = bankb(6)
                    for kc in range(2):
                        tr(tp2[:, kc * BS:(kc + 1) * BS], pbf[:BS, b, kc * 128:(kc + 1) * 128], ident[:BS, :BS], [("pbf", b), "ident"], [BK(6)])
                    yield
                    cp_("act", peT[:, :, :BS], tp2[:, 0:2 * BS].rearrange("p (k t) -> p k t", k=2), [BK(6)], ["peT"])
                    pq = pair(2)
                    for half in range(2):
                        for kc in range(2):
                            mm(pq[:BS, half * 512:(half + 1) * 512], peT[:, kc, :BS], wp[:, kc, half * 512:(half + 1) * 512],
                               kc == 0, kc == 1, ["peT", "wp"], [BK(4), BK(5)])
                    act(sg[:BS], gp[:BS], AF.Sigmoid, [BK(2), BK(3)], ["sg"])
                    yield
                    tt(ple[:BS], pq[:BS], sg[:BS], ALU.mult, [BK(4), BK(5), "sg"], [("ple", k)])
                    yield
                    act(junk[:BS], ple[:BS], AF.Square, [("ple", k)], ["junk", ("ss3", 2)], accum=ss3[:BS, 2:3])
                    yield
                    rstd_pool(rstd3[:, 2:3], ss3[:, 2:3], BS, ("ss3", 2))
                    yield
                    stt(ple[:BS], ple[:BS], rstd3[:BS, 2:3], gple_bc[:BS], ALU.mult, ALU.mult, [("ple", k), ("ss3", 2), "gple"], [("ple", k)])
                    tt(ple[:BS], ple[:BS], hres[:BS], ALU.add, [("ple", k), ("hres", k)], [("ple", k)])
                    dma("pool", q.y[tok0:tok0 + BS, :], ple[:BS], [("ple", k)], [], ("yout", k))

                run_interleaved([T1(0)])
                for b in range(nb):
                    run_interleaved([T1(b + 1) if b + 1 < nb else None, T2(b),
                                     do_front(j + 1, b) if j + 1 < ntile else None])
            for r_ in range(3):
                dma("pool", q.cout[r_].rearrange("(c p o) -> p c o", p=128, o=1), hist[:, :, r_:r_ + 1], ["hist"], [], "cout", slow=True)
            dma("pool", q.hout.rearrange("(c p o) -> p c o", p=128, o=1), hstate.rearrange("p (c o) -> p c o", o=1), ["hstate"], [], "hout", slow=True)
        print("ops", len(s.ops))
        s.emit()
        print("nsem", s.nsem)
    return nc


_CACHE = {}


def _get_nc(S):
    if S not in _CACHE:
        _CACHE[S] = build(S)
    return _CACHE[S]


def kernel(x_prompt, x_sample, p_prompt, p_sample, cache_k, cache_v, cache_logf, state_h, state_conv,
           w_in, conv_w, conv_b, w_rg_a, b_rg_a, w_rg_x, b_rg_x, a_param, b_f, w_branch, w_o,
           g_pre, g_post, w_ple_gate, w_ple_proj, g_ple, _ncores=8):
    f = lambda a: np.ascontiguousarray(np.asarray(a, dtype=np.float32))
    B, S, _ = x_prompt.shape
    n = _ncores
    nc = _get_nc(S)
    shared = dict(w_in=f(w_in[0]), conv_w=f(conv_w[0]), conv_b=f(conv_b[0]), w_rg_a=f(w_rg_a[0]), b_rg_a=f(b_rg_a[0]),
                  w_rg_x=f(w_rg_x[0]), b_rg_x=f(b_rg_x[0]), a_param=f(a_param[0]), b_f=f(b_f[0]),
                  w_branch=f(w_branch[0]), w_o=f(w_o[0]), g_pre=f(g_pre[0]), g_post=f(g_post[0]),
                  w_pg=f(w_ple_gate[0]), w_pp=f(w_ple_proj[0]), g_ple=f(g_ple[0]))
    in_maps = []
    for c in range(n):
        m = dict(shared)
        m["xp"] = f(x_prompt[c]); m["pp"] = f(p_prompt[0, c])
        m["xs"] = f(x_sample[2 * c:2 * c + 2]); m["psm"] = f(p_sample[0, 2 * c:2 * c + 2])
        m["ck"] = f(np.asarray(cache_k[0, 2 * c:2 * c + 2]).reshape(2, PAST, D))
        m["cv"] = f(np.asarray(cache_v[0, 2 * c:2 * c + 2]).reshape(2, PAST, D))
        m["clf"] = f(cache_logf[0, 2 * c:2 * c + 2])
        m["sh"] = f(state_h[0, 2 * c:2 * c + 2]); m["sc"] = f(state_conv[0, 2 * c:2 * c + 2])
        in_maps.append(m)
    res = run_bass_kernel_spmd(nc, in_maps, core_ids=list(range(n))).results
    cat = lambda k: np.stack([np.asarray(r[k]) for r in res], 0)
    cat2 = lambda k: np.concatenate([np.asarray(r[k]) for r in res], 0)
    y_p = cat("yp"); y_s = cat2("ys")
    k_p = cat("kp").reshape(1, n, S, H, 64); v_p = cat("vp").reshape(1, n, S, H, 64)
    lf_p = cat("lfp").reshape(1, n, S, H)
    h_p = cat("hp").reshape(1, n, D); c_p = cat("cp").reshape(1, n, 3, D)
    k_s = cat2("ks").reshape(1, 2 * n, DEC, H, 64); v_s = cat2("vs").reshape(1, 2 * n, DEC, H, 64)
    lf_s = cat2("lfs").reshape(1, 2 * n, DEC, H)
    h_s = cat2("hs").reshape(1, 2 * n, D); c_s = cat2("cs").reshape(1, 2 * n, 3, D)
    return (y_p, y_s, k_p, v_p, lf_p, h_p, c_p, k_s, v_s, lf_s, h_s, c_s)
```

```python
import contextlib
import numpy as np
import concourse.bass as bass
import concourse.mybir as mybir
from concourse.bass_utils import run_bass_kernel_spmd

F32 = mybir.dt.float32
BF16 = mybir.dt.bfloat16
AF = mybir.ActivationFunctionType
ALU = mybir.AluOpType

D = 1024
H = 16
PLE = 256
PAST = 1024
DEC = 64
EPS = 1e-6
OFF = dict(xa=0, ga=1024, q=2048, k=3072, v=4096, gb=5120, fl=6144, ma=6160, mb=7184)
ARENA_BYTES = 204 * 1024


class Sched:
    ENG = ("sp", "act", "dve", "pool", "pe")

    def __init__(self, nc):
        self.nc = nc
        self.ops = []

    def op(self, eng, fn, r=(), w=(), dma=None):
        self.ops.append(dict(eng=eng, fn=fn, r=tuple(r), w=tuple(w), dma=dma, bar=False))

    def barrier(self):
        for e in self.ENG:
            self.ops.append(dict(eng=e, fn=None, r=(), w=(), dma=None, bar=True))

    def emit(self):
        nc = self.nc
        ops = self.ops
        n = len(ops)
        last_w = {}
        readers = {}
        deps = [None] * n
        last_eng = {}
        last_dma = {}
        for i, o in enumerate(ops):
            if o["bar"]:
                dd = [j for e, j in last_eng.items() if e != o["eng"]] + list(last_dma.values())
                deps[i] = dd
                continue
            d = {}
            for k in o["r"]:
                if k in last_w:
                    d[last_w[k]] = True
            for k in o["w"]:
                if k in last_w:
                    d.setdefault(last_w[k], False)
                for rr in readers.get(k, ()):
                    d.setdefault(rr, False)
            for k in o["r"]:
                readers.setdefault(k, []).append(i)
            for k in o["w"]:
                last_w[k] = i
                readers[k] = []
            d.pop(i, None)
            dd = []
            for j, raw in d.items():
                p = ops[j]
                if p["dma"] is None and o["dma"] is None and p["eng"] == o["eng"]:
                    if o["eng"] == "pe" or not raw:
                        continue
                dd.append(j)
            deps[i] = dd
            if o["dma"] is not None:
                last_dma[o["dma"]] = i
            else:
                last_eng[o["eng"]] = i
        need = [False] * n
        for i in range(n):
            for j in deps[i]:
                need[j] = True
        cnt = {}
        semkey = [None] * n
        semval = [0] * n
        for i, o in enumerate(ops):
            if o["bar"]:
                continue
            if o["dma"] is not None:
                k = ("dma", o["dma"])
                cnt[k] = cnt.get(k, 0) + 16
                semkey[i] = k
                semval[i] = cnt[k]
            elif need[i]:
                k = ("eng", o["eng"])
                cnt[k] = cnt.get(k, 0) + 1
                semkey[i] = k
                semval[i] = cnt[k]
        keys = sorted(set(k for k in semkey if k is not None), key=str)
        self.nsem = len(keys)
        with contextlib.ExitStack() as es:
            sems = {}
            for idx, k in enumerate(keys):
                sems[k] = es.enter_context(nc.semaphore("s%d" % idx))
            block = es.enter_context(nc.Block())
            per_eng = {e: [] for e in self.ENG}
            for i, o in enumerate(ops):
                per_eng[o["eng"]].append(i)
            final = [(k, cnt[k]) for k in keys if k[0] == "dma"]

            def run(engobj, ename):
                waited = {}
                for i in per_eng[ename]:
                    o = ops[i]
                    w = {}
                    for j in deps[i]:
                        k = semkey[j]
                        w[k] = max(w.get(k, 0), semval[j])
                    for k, v in w.items():
                        if waited.get(k, 0) >= v:
                            continue
                        engobj.wait_ge(sems[k], v)
                        waited[k] = v
                    if o["bar"]:
                        continue
                    inst = o["fn"](engobj)
                    if semkey[i] is not None:
                        inst.then_inc(sems[semkey[i]], 16 if o["dma"] is not None else 1)
                if ename == "sp":
                    for k, v in final:
                        engobj.wait_ge(sems[k], v)

            @block.sync
            def _(e):
                run(e, "sp")

            @block.scalar
            def _(e):
                run(e, "act")

            @block.vector
            def _(e):
                run(e, "dve")

            @block.gpsimd
            def _(e):
                run(e, "pool")

            @block.tensor
            def _(e):
                run(e, "pe")


class Seq:
    pass


def build(S):
    nc = bass.Bass("TRN2", target_bir_lowering=False)
    di = lambda name, shape: nc.dram_tensor(name, list(shape), F32, kind="ExternalInput").ap()
    do = lambda name, shape: nc.dram_tensor(name, list(shape), F32, kind="ExternalOutput").ap()
    ds = lambda name, shape: nc.dram_tensor(name, list(shape), BF16, kind="Internal").ap()

    xp = di("xp", (S, D)); pp = di("pp", (S, PLE))
    xs = di("xs", (2, DEC, D)); psm = di("psm", (2, DEC, PLE))
    ck = di("ck", (2, PAST, D)); cv = di("cv", (2, PAST, D)); clf = di("clf", (2, PAST, H))
    sh = di("sh", (2, D)); sc = di("sc", (2, 3, D))
    w_in = di("w_in", (D, 8208)); conv_w = di("conv_w", (4, D)); conv_b = di("conv_b", (D,))
    w_rg_a = di("w_rg_a", (16, 64, 64)); b_rg_a = di("b_rg_a", (D,))
    w_rg_x = di("w_rg_x", (16, 64, 64)); b_rg_x = di("b_rg_x", (D,))
    a_param = di("a_param", (D,)); b_f = di("b_f", (H,))
    w_branch = di("w_branch", (2, D, D)); w_o = di("w_o", (D, D))
    g_pre = di("g_pre", (D,)); g_post = di("g_post", (D,))
    w_pg = di("w_pg", (D, D)); w_pp = di("w_pp", (PLE, D)); g_ple = di("g_ple", (D,))

    yp = do("yp", (S, D)); ys = do("ys", (2, DEC, D))
    kp = do("kp", (S, D)); vp = do("vp", (S, D)); lfp = do("lfp", (S, H))
    hp = do("hp", (D,)); cp = do("cp", (3, D))
    ks = do("ks", (2, DEC, D)); vs = do("vs", (2, DEC, D)); lfs = do("lfs", (2, DEC, H))
    hs_o = do("hs", (2, D)); cs = do("cs", (2, 3, D))

    wS3 = ds("wS3", (40, 128, 1024))
    wR3 = ds("wR3", (34, 128, 1024))

    seqs = []
    q = Seq(); q.name = "p"; q.L = S; q.P = 0; q.T = 512; q.BS = 128
    q.x = xp; q.p = pp; q.y = yp; q.kout = kp; q.vout = vp; q.lfout = lfp; q.hout = hp; q.cout = cp
    q.h0 = None; q.c0 = None; q.pk = None
    seqs.append(q)
    for b in range(2):
        q = Seq(); q.name = "s%d" % b; q.L = DEC; q.P = PAST; q.T = DEC; q.BS = DEC
        q.x = xs[b]; q.p = psm[b]; q.y = ys[b]; q.kout = ks[b]; q.vout = vs[b]; q.lfout = lfs[b]
        q.hout = hs_o[b]; q.cout = cs[b]; q.h0 = sh[b]; q.c0 = sc[b]
        q.pk = ck[b]; q.pv = cv[b]; q.plf = clf[b]
        seqs.append(q)
    for q in seqs:
        q.Lk = q.P + q.L
        q.nkt = (q.Lk + 127) // 128
        q.nblk = q.P // 128 + q.L // q.BS
        q.qS = ds("qS" + q.name, (8, 128, q.L))
        q.kS = ds("kS" + q.name, (8, 128, q.Lk))
        q.vS = ds("vS" + q.name, (8, q.nkt * 128, 192))
        q.yS = ds("yS" + q.name, (8, 128, q.L))

    with contextlib.ExitStack() as es:
        arena = es.enter_context(nc.sbuf_tensor("arena", [128, ARENA_BYTES // 4], F32))
        pst = [es.enter_context(nc.psum_tensor("ps%d" % i, [128, 1024], F32)) for i in range(4)]
        s = Sched(nc)
        st = dict(off=0)

        def alloc(shape, dt):
            n = 1
            for v in shape[1:]:
                n *= v
            nb = n * (4 if dt == F32 else 2)
            nb = (nb + 31) // 32 * 32
            o = st["off"]
            assert o + nb <= ARENA_BYTES, ("arena overflow", o + nb)
            st["off"] = o + nb
            ap = arena[:, o // 4:(o + nb) // 4]
            if dt != F32:
                ap = ap.bitcast(dt)
            ap = ap[:, 0:n]
            if len(shape) == 3:
                ap = ap.rearrange("p (a b) -> p a b", a=shape[1])
            elif len(shape) == 4:
                ap = ap.rearrange("p (a b c) -> p a b c", a=shape[1], b=shape[2])
            return ap

        def bank(i):
            return pst[i // 2][:, (i % 2) * 512:(i % 2) * 512 + 512]

        def bankb(i):
            return bank(i).bitcast(BF16)

        def pair(i):
            return pst[i][:, :]

        BK = lambda i: ("bank", i)

        def dma(eng, out, in_, r, w, key, slow=False):
            if slow:
                s.op(eng, lambda e: e.dma_start(out=out, in_=in_, allow_slow_non_contiguous=True), r=r, w=w, dma=key)
            else:
                s.op(eng, lambda e: e.dma_start(out=out, in_=in_), r=r, w=w, dma=key)

        def mm(out, lhsT, rhs, start, stop, r, w):
            s.op("pe", lambda e: e.matmul(out, lhsT=lhsT, rhs=rhs, start=start, stop=stop), r=r, w=w)

        def tr(out, in_, ident, r, w):
            s.op("pe", lambda e: e.transpose(out, in_, ident), r=r, w=w)

        def act(out, in_, func, r, w, bias=None, scale=None, accum=None, eng="act"):
            kw = {}
            if bias is not None:
                kw["bias"] = bias
            if scale is not None:
                kw["scale"] = scale
            if accum is not None:
                kw["accum_out"] = accum
            s.op("act", lambda e: e.activation(out=out, in_=in_, func=func, **kw), r=r, w=w)

        def tt(out, a, b, op, r, w, eng="dve"):
            s.op(eng, lambda e: e.tensor_tensor(out=out, in0=a, in1=b, op=op), r=r, w=w)

        def ts(out, a, s1, s2, op0, op1, r, w, eng="dve"):
            if op1 is None:
                s.op(eng, lambda e: e.tensor_scalar(out=out, in0=a, scalar1=s1, scalar2=None, op0=op0), r=r, w=w)
            else:
                s.op(eng, lambda e: e.tensor_scalar(out=out, in0=a, scalar1=s1, scalar2=s2, op0=op0, op1=op1), r=r, w=w)

        def stt(out, a, sc_, b, op0, op1, r, w, eng="dve"):
            s.op(eng, lambda e: e.scalar_tensor_tensor(out=out, in0=a, scalar=sc_, in1=b, op0=op0, op1=op1), r=r, w=w)

        def cp_(eng, out, in_, r, w):
            if eng == "act":
                s.op("act", lambda e: e.activation(out=out, in_=in_, func=AF.Copy), r=r, w=w)
            else:
                s.op(eng, lambda e: e.tensor_copy(out=out, in_=in_), r=r, w=w)

        def memset(ap, val, w, eng="pool"):
            s.op(eng, lambda e: e.memset(ap, val), w=w)

        ident = alloc([128, 128], BF16)
        onesb = alloc([128, 512], BF16)
        U = alloc([128, 128], F32)
        onesf = alloc([128, 128], F32)
        gpre_bc = alloc([128, D], F32); gpost_bc = alloc([128, D], F32); gple_bc = alloc([128, D], F32)
        bf_bc = alloc([128, H], F32)
        cw = alloc([128, 8, 4], F32); cb = alloc([128, 8], F32)
        bra = alloc([128, 8], F32); brx = alloc([128, 8], F32)
        nsp8 = alloc([128, 8], F32); psp8 = alloc([128, 8], F32); apar = alloc([128, 8], F32)
        junk = alloc([128, D], BF16)
        mhalf = alloc([128, 1], F32)
        memset(onesb, 1.0, ["onesb"])
        memset(onesf, 1.0, ["onesf"])
        memset(mhalf, -0.5, ["mhalf"])
        s.op("pool", lambda e: e.affine_select(out=ident, in_=onesb[:, 0:128], pattern=[[1, 128]], compare_op=ALU.is_equal,
                                               fill=0.0, base=0, channel_multiplier=-1), r=["onesb"], w=["ident"])
        s.op("pool", lambda e: e.affine_select(out=U, in_=onesf, pattern=[[1, 128]], compare_op=ALU.is_ge,
                                               fill=0.0, base=0, channel_multiplier=-1), r=["onesf"], w=["U"])
        dma("sp", gpre_bc, g_pre.partition_broadcast(128), [], ["gpre"], "c0")
        dma("sp", gpost_bc, g_post.partition_broadcast(128), [], ["gpost"], "c1")
        dma("sp", gple_bc, g_ple.partition_broadcast(128), [], ["gple"], "c2")
        dma("sp", bf_bc, b_f.partition_broadcast(128), [], ["bfbc"], "c3")
        for r_ in range(4):
            dma("sp", cw[:, :, r_:r_ + 1], conv_w[r_].rearrange("(c p o) -> p c o", p=128, o=1), [], ["cw"], "c4", slow=True)
        for tdst, tsrc, kk in ((cb, conv_b, "cb"), (bra, b_rg_a, "bra"), (brx, b_rg_x, "brx"), (apar, a_param, "apar")):
            dma("sp", tdst.rearrange("p (c o) -> p c o", o=1), tsrc.rearrange("(c p o) -> p c o", p=128, o=1), [], [kk], ("c5", kk), slow=True)
        act(nsp8, apar, AF.Exp, ["apar"], ["nsp8"], scale=-1.0)
        act(nsp8, nsp8, AF.Ln, ["nsp8"], ["nsp8"], bias=1.0)
        ts(psp8, nsp8, 8.0, None, ALU.mult, None, ["nsp8"], ["psp8"])
        ts(nsp8, nsp8, -8.0, None, ALU.mult, None, ["nsp8", "psp8"], ["nsp8"])
        hbra = alloc([128, 8], F32); hbrx = alloc([128, 8], F32)
        hnsp8 = alloc([128, 8], F32); hpsp8 = alloc([128, 8], F32)
        ts(hbra, bra, 0.5, None, ALU.mult, None, ["bra"], ["hbra"])
        ts(hbrx, brx, 0.5, None, ALU.mult, None, ["brx"], ["hbrx"])
        ts(hnsp8, nsp8, 0.5, None, ALU.mult, None, ["nsp8"], ["hnsp8"])
        ts(hpsp8, psp8, 0.5, None, ALU.mult, None, ["psp8"], ["hpsp8"])
        const_end = st["off"]

        for q in seqs:
            q.logf = alloc([128, q.nblk, H], F32)
            q.cumT = alloc([128, q.nblk, H], F32)
            q.carry = alloc([128, q.nblk + 1, H], F32)
            memset(q.cumT, 0.0, [("cumT", q.name)])
            memset(q.carry, 0.0, [("carry", q.name)])
            memset(q.logf, 0.0, [("logf", q.name)])
        persist_end = st["off"]

        def rstd_pool(rstd, ss, rows, kss):
            ts(rstd[:rows], ss[:rows], 1.0 / D, EPS, ALU.mult, ALU.add, [kss], [kss], eng="pool")
            tt(rstd[:rows], rstd[:rows], mhalf[:rows], ALU.pow, [kss, "mhalf"], [kss], eng="pool")

        def front_gen(q, tok0, xsl, kx, ss, rstd, kss, xn, kxn, xnT, kxnT, b, tpb):
            BS = q.BS
            dma("sp", xsl[:BS], q.x[tok0:tok0 + BS, :], [], [kx], kx)
            act(junk[:BS], xsl[:BS], AF.Square, [kx], ["junk", kss], accum=ss[:BS, 0:1])
            yield
            rstd_pool(rstd[:, 0:1], ss[:, 0:1], BS, kss)
            yield
            stt(xn[:BS], xsl[:BS], rstd[:BS, 0:1], gpre_bc[:BS], ALU.mult, ALU.mult, [kx, kss, "gpre"], [kxn])
            yield
            tp = bankb(tpb)
            for kc in range(8):
                tr(tp[:, kc * BS:(kc + 1) * BS], xn[:BS, kc * 128:(kc + 1) * 128], ident[:BS, :BS],
                   [kxn, "ident"], [BK(tpb)])
            yield
            cp_("act", xnT[:, :, b * BS:(b + 1) * BS], tp[:, 0:8 * BS].rearrange("p (k t) -> p k t", k=8),
                [BK(tpb)], [kxnT])

        def front(*a):
            for _ in front_gen(*a):
                pass

        def run_interleaved(gens):
            gens = [g for g in gens if g is not None]
            while gens:
                for g in list(gens):
                    try:
                        next(g)
                    except StopIteration:
                        gens.remove(g)

        wq = alloc([128, 8, D], BF16); wk = alloc([128, 8, D], BF16); wv = alloc([128, 8, D], BF16)
        wf = alloc([128, 8, H], BF16)
        wstage = alloc([128, 8, D], F32)
        for wt, pn in ((wk, "k"), (wv, "v"), (wq, "q")):
            for kc in range(8):
                dma("sp", wstage[:, kc, :], w_in[kc * 128:(kc + 1) * 128, OFF[pn]:OFF[pn] + D], [], [("wstg", kc)], ("wstg", kc))
                cp_("dve" if kc % 2 == 0 else "act", wt[:, kc, :], wstage[:, kc, :], [("wstg", kc)], [("w1", pn, kc)])
        dma("pool", wf, w_in[:, OFF["fl"]:OFF["fl"] + H].rearrange("(kc p) n -> p kc n", p=128), [], [("w1", "fl")], ("w1", "fl"))
        for pi, pn in enumerate(("xa", "ga", "gb", "ma", "mb")):
            for c in range(8):
                src = w_in[:, OFF[pn] + c * 128:OFF[pn] + c * 128 + 128].rearrange("(kc p) n -> p kc n", p=128)
                dma("pool", wS3[pi * 8 + c].rearrange("p (kc n) -> p kc n", kc=8), src, [], [("wS3", pi, c)], "c7")
        for wi, srcw in enumerate((w_branch[0], w_branch[1], w_o, w_pg)):
            for kc in range(8):
                dma("pool", wR3[wi * 8 + kc], srcw[kc * 128:(kc + 1) * 128, :], [], [("wR3", wi * 8 + kc)], "c8")
        for kc in range(2):
            dma("pool", wR3[32 + kc], w_pp[kc * 128:(kc + 1) * 128, :], [], [("wR3", 32 + kc)], "c8")
        NX = 2
        xring = [alloc([128, D], F32) for _ in range(NX)]
        ss1 = alloc([128, 2], F32); rstd1 = alloc([128, 2], F32)
        xn1 = [alloc([128, D], BF16) for _ in range(2)]
        xnT1 = [alloc([128, 8, 512], BF16) for _ in range(2)]
        ktok = [alloc([128, D], F32) for _ in range(2)]
        vtok = [alloc([128, D], F32) for _ in range(2)]
        vaug = [alloc([128, 8, 3, 64], BF16) for _ in range(2)]
        for i in range(2):
            cp_("pool", vaug[i][:, :, 1, :], onesb[:, 0:512].rearrange("p (a b) -> p a b", a=8), ["onesb"], [("vaug", i)])
        stK = [alloc([128, 8, 512], BF16) for _ in range(2)]
        stQ = [alloc([128, 8, 512], BF16) for _ in range(2)]
        flb = alloc([128, H], F32); e1 = alloc([128, H], F32)
        ckb = alloc([128, D], BF16); cvb = alloc([128, D], BF16)
        s1_cnt = dict(blk=0, tile=0, va=0, kv=0)

        def cumsum_block(q, blk, BS):
            lf = q.logf[:BS, blk, :]
            ps = bank(4)
            mm(ps[:BS, 16:32], U[:BS, :BS], lf, True, True, ["U", ("logf", q.name)], [BK(4)])
            mm(ps[:, 32:48], onesf[:BS, :], lf, True, True, ["onesf", ("logf", q.name)], [BK(4)])
            tt(q.cumT[:BS, blk, :], ps[:BS, 16:32], q.carry[:BS, blk, :], ALU.add, [BK(4), ("carry", q.name)], [("cumT", q.name)])
            tt(q.carry[:, blk + 1, :], ps[:, 32:48], q.carry[:, blk, :], ALU.add, [BK(4), ("carry", q.name)], [("carry", q.name)])

        def vaug_store(q, srcap, rows, kpos0, srckeys):
            i = s1_cnt["va"] % 2
            s1_cnt["va"] += 1
            va = vaug[i]
            cp_("dve", va[:rows, :, 0:3:2, :], srcap[:rows].rearrange("p (h e d) -> p h e d", h=8, e=2),
                srckeys, [("vaug", i)])
            dma("sp", q.vS[:, kpos0:kpos0 + rows, :].rearrange("h t c -> t h c"),
                va[:rows].rearrange("p h e d -> p h (e d)"), [("vaug", i)], [("vS", q.name)], ("vaugst", i))

        for q in seqs:
            BS, T = q.BS, q.T
            nb = T // BS
            if q.P:
                npb = q.P // 128
                dma("sp", q.logf[:, 0:npb, :], q.plf.rearrange("(i p) h -> p i h", p=128), [], [("logf", q.name)], ("plf", q.name), slow=True)
                for blk in range(npb):
                    dma("pool", ckb, q.pk[blk * 128:(blk + 1) * 128, :], [], ["ckb"], "ckb")
                    tp = bankb(6)
                    for kc in range(8):
                        tr(tp[:, kc * 128:(kc + 1) * 128], ckb[:, kc * 128:(kc + 1) * 128], ident, ["ckb", "ident"], [BK(6)])
                    sl = s1_cnt["tile"] % 2
                    s1_cnt["tile"] += 1
                    cp_("act", stK[sl][:, :, 0:128], tp.rearrange("p (k t) -> p k t", k=8), [BK(6)], [("stk", sl)])
                    dma("sp", q.kS[:, :, blk * 128:(blk + 1) * 128].rearrange("h p t -> p h t"), stK[sl][:, :, 0:128],
                        [("stk", sl)], [("kS", q.name)], ("stkst_past", sl))
                    dma("pool", cvb, q.pv[blk * 128:(blk + 1) * 128, :], [], ["cvb"], "cvb")
                    vaug_store(q, cvb, 128, blk * 128, ["cvb"])
                    cumsum_block(q, blk, 128)
            ntile1 = q.L // T
            slots = []
            for j in range(ntile1):
                slots.append(s1_cnt["tile"] % 2)
                s1_cnt["tile"] += 1

            def s1_front(j, b):
                g = s1_cnt["blk"]
                s1_cnt["blk"] += 1
                g2 = g % 2
                sl = slots[j]
                return front_gen(q, j * T + b * BS, xring[g % NX], ("x1", g % NX), ss1[:, g2:g2 + 1], rstd1[:, g2:g2 + 1], ("ss1", g2),
                                 xn1[g2], ("xn1", g2), xnT1[sl], ("xnT1", sl, b), b, 6)

            def drain(g):
                for _ in g:
                    pass

            for b in range(nb):
                drain(s1_front(0, b))
            deferred = []
            for j in range(ntile1):
                sl = slots[j]
                xnT = xnT1[sl]
                for b in range(nb):
                    g2 = s1_cnt["kv"] % 2
                    s1_cnt["kv"] += 1
                    tok0 = j * T + b * BS
                    kxnT = ("xnT1", sl, b)
                    blk = q.P // 128 + tok0 // BS
                    xk = [kxnT]
                    lhs = lambda kc: xnT[:, kc, b * BS:(b + 1) * BS]
                    for wt, pn, dst, pr, outap, ev in ((wk, "k", ktok[g2], 0, q.kout, "act"), (wv, "v", vtok[g2], 1, q.vout, "dve")):
                        for half in range(2):
                            ps = bank(pr * 2 + half)
                            for kc in range(8):
                                mm(ps[:BS, :], lhs(kc), wt[:, kc, half * 512:(half + 1) * 512], kc == 0, kc == 7,
                                   xk + [("w1", pn, kc)], [BK(pr * 2 + half)])
                            cp_(ev, dst[:BS, half * 512:(half + 1) * 512], ps[:BS, :], [BK(pr * 2 + half)], [(pn + "tok", g2)])
                        dma("act" if pn == "k" else "sp", outap[tok0:tok0 + BS, :], dst[:BS], [(pn + "tok", g2)], [], (pn + "tokst", g2))
                    vaug_store(q, vtok[g2], BS, q.P + tok0, [("vtok", g2)])
                    ps = bank(4)
                    for kc in range(8):
                        mm(ps[:BS, 0:H], lhs(kc), wf[:, kc, :], kc == 0, kc == 7, xk + [("w1", "fl")], [BK(4)])
                    for f in deferred:
                        f()
                    deferred = []
                    tt(flb[:BS], ps[:BS, 0:H], bf_bc[:BS], ALU.add, [BK(4), "bfbc"], ["flb"])
                    act(e1[:BS], flb[:BS], AF.Exp, ["flb"], ["e1"], scale=-1.0)
                    act(e1[:BS], e1[:BS], AF.Ln, ["e1"], ["e1"], bias=1.0)
                    ts(q.logf[:BS, blk, :], e1[:BS], -1.0, None, ALU.mult, None, ["e1"], [("logf", q.name)])
                    deferred.append((lambda blk=blk: cumsum_block(q, blk, BS)))
                    if j + 1 < ntile1:
                        fg = s1_front(j + 1, b)
                        next(fg); next(fg); next(fg)
                        deferred.append((lambda fg=fg: drain(fg)))
                xkall = [("xnT1", sl, b) for b in range(nb)]
                for wt, pn, stg, scr, pos0 in ((wk, "k", stK[sl], q.kS, q.P + j * T), (wq, "q", stQ[sl], q.qS, j * T)):
                    for cc in range(8):
                        bi = 5 + 2 * (cc % 2)
                        ps = bank(bi)
                        for kc in range(8):
                            mm(ps[:, :T], wt[:, kc, cc * 128:(cc + 1) * 128], xnT[:, kc, :T], kc == 0, kc == 7,
                               xkall + [("w1", pn, kc)], [BK(bi)])
                        if pn == "q":
                            ts(stg[:, cc, :T], ps[:, :T], 0.125, None, ALU.mult, None, [BK(bi)], [("st" + pn, sl)])
                        else:
                            cp_("act", stg[:, cc, :T], ps[:, :T], [BK(bi)], [("st" + pn, sl)])
                    dma("act" if pn == "k" else "sp", scr[:, :, pos0:pos0 + T].rearrange("h p t -> p h t"), stg[:, :, :T],
                        [("st" + pn, sl)], [(pn + "S", q.name)], ("st" + pn + "st", sl))
            for f in deferred:
                f()
            deferred = []
            npb = q.P // 128
            nbl = q.L // BS
            step = 16
            for i0 in range(0, nbl, step):
                i1 = min(nbl, i0 + step)
                dma("pool", q.lfout[i0 * BS:i1 * BS, :].rearrange("(i p) h -> p i h", p=BS), q.logf[:BS, npb + i0:npb + i1, :],
                    [("logf", q.name)], [], "lfout", slow=True)

        s.barrier()
        st["off"] = persist_end
        Lmax = max(q.L for q in seqs)
        Lkmax = max(q.Lk for q in seqs)
        nktmax = max(q.nkt for q in seqs)
        masks = alloc([128, 4, 512], BF16)
        for m in range(4):
            (lambda m: s.op("pool", lambda e: e.affine_select(out=masks[:, m, :], in_=onesb, pattern=[[1, 512]],
                                                              compare_op=ALU.is_ge, fill=0.0, base=-128 * m,
                                                              channel_multiplier=-1), r=["onesb"], w=["masks"]))(m)
        kTb = [alloc([128, Lkmax], BF16) for _ in range(2)]
        qTb = [alloc([128, Lmax], BF16) for _ in range(2)]
        vab = [alloc([128, nktmax, 192], BF16) for _ in range(2)]
        ybst = alloc([128, Lmax], BF16)
        pT = [alloc([128, 512], BF16) for _ in range(4)]
        rs = [alloc([128, 512], F32) for _ in range(3)]; bcs = alloc([128, 512], F32)
        biasb = [alloc([128, 64], F32) for _ in range(2)]
        for i in range(3):
            memset(rs[i], 1.0, [("rs", i)])
        OB = (4, 5, 7)
        for i in range(2):
            memset(kTb[i][0:1, :], 1.0, [("kT0", i)])
            memset(qTb[i][0:1, :], 0.0, [("qT0", i)])
        for _ in range(24):
            mm(bank(7)[:, :], masks[:, 1, 0:128], masks[:, 2, :], True, True, ["masks"], [BK(7)])
        cnt2 = dict(hd=0, pr=0, s=0, o=0, b=0)
        LA = 2
        DEFER = 7

        def head_load(q, hp_, hh, sl):
            dma("sp", kTb[sl][1:65, :q.Lk], q.kS[hp_, 64 * hh:64 * hh + 64, :], [("kS", q.name)], [("kT", sl)], ("kT", sl))
            dma("sp", qTb[sl][1:65, :q.L], q.qS[hp_, 64 * hh:64 * hh + 64, :], [("qS", q.name)], [("qT", sl)], ("qT", sl))

        def dq_chunks(q, h, sl):
            BS, Tq = q.BS, q.T
            nbt = Tq // BS
            chunks = []
            for j in range(q.L // Tq):
                def chunk(j=j):
                    cref = q.P // 128 + ((j + 1) * Tq) // BS
                    for b in range(nbt):
                        blk = j * nbt + b
                        gblk = q.P // 128 + blk
                        mm(bank(6)[0:1, b * BS:(b + 1) * BS], q.logf[:BS, gblk, h:h + 1], U[:BS, :BS], True, True,
                           [("logf", q.name), "U"], [BK(6)])
                    for b in range(nbt):
                        blk = j * nbt + b
                        gblk = q.P // 128 + blk
                        ts(qTb[sl][0:1, blk * BS:(blk + 1) * BS], bank(6)[0:1, b * BS:(b + 1) * BS],
                           q.carry[0:1, gblk, h:h + 1], q.carry[0:1, cref, h:h + 1], ALU.add, ALU.subtract,
                           [BK(6), ("carry", q.name)], [("qT0", sl)])
                chunks.append(chunk)
            return chunks

        heads = [(q, hp_, hh) for q in seqs for hp_ in range(8) for hh in range(2)]
        q_, hp0, hh0 = heads[0]
        head_load(q_, hp0, hh0, 0)
        for ch in dq_chunks(q_, 2 * hp0 + hh0, 0):
            ch()
        for hidx, (q, hp_, hh) in enumerate(heads):
            Tq = q.T
            NSUB = max(1, Tq // 128)
            W = Tq // NSUB
            sl = hidx % 2
            vsl = (hidx // 2) % 2
            kT, qT, va = kTb[sl], qTb[sl], vab[vsl]
            h = 2 * hp_ + hh
            if hh == 0:
                dma("sp", va[:, :q.nkt, :], q.vS[hp_].rearrange("(i p) c -> p i c", p=128), [("vS", q.name)], [("va", vsl)], ("va", vsl))
            nxt = heads[hidx + 1] if hidx + 1 < len(heads) else None
            nxt_chunks = []
            if nxt is not None:
                head_load(nxt[0], nxt[1], nxt[2], 1 - sl)
                nxt_chunks = dq_chunks(nxt[0], 2 * nxt[1] + nxt[2], 1 - sl)
            ev = []
            for j in range(q.L // Tq):
                q0 = j * Tq
                nk = (q.P + q0 + Tq + 127) // 128
                bsl = cnt2["b"] % 2
                cnt2["b"] += 1
                ob = OB[cnt2["o"] % 3]
                cnt2["o"] += 1
                ev.append(("pre", dict(h=h, q0=q0, nk=nk, bsl=bsl, j=j)))
                for i in range(nk):
                    sn = cnt2["s"]
                    cnt2["s"] += 1
                    ev.append(("pair", dict(hh=hh, h=h, q0=q0, nk=nk, i=i, bsl=bsl, ob=ob, sn=sn)))
                ev.append(("post", dict(hh=hh, q0=q0, ob=ob)))
            pairs = [d for k_, d in ev if k_ == "pair"]

            def emit_qk(d):
                q0, i, sn = d["q0"], d["i"], d["sn"]
                rows = min(128, q.Lk - 128 * i)
                sub0 = 0
                while q.P + q0 + (sub0 + 1) * W - 1 < 128 * i:
                    sub0 += 1
                d["rows"], d["sub0"] = rows, sub0
                c0 = sub0 * W
                sb_ = sn % 4
                mm(bank(sb_)[:rows, c0:Tq], kT[0:65, 128 * i:128 * i + rows], qT[0:65, q0 + c0:q0 + Tq], True, True,
                   [("kT", sl), ("kT0", sl), ("qT", sl), ("qT0", sl)], [BK(sb_)])

            def emit_pair(d):
                hh, q0, i, sn, nk, ob = d["hh"], d["q0"], d["i"], d["sn"], d["nk"], d["ob"]
                rows, sub0 = d["rows"], d["sub0"]
                c0 = sub0 * W
                sb_ = sn % 4
                pi = sn % 4
                sps = bank(sb_)
                bb = biasb[d["bsl"]]
                bkey = ("bias", d["bsl"])
                act(pT[pi][:rows, c0:Tq], sps[:rows, c0:Tq], AF.Exp, [BK(sb_), bkey], [("pT", pi)], bias=bb[:rows, i:i + 1])
                off = q.P + q0 + sub0 * W - 128 * i
                if off < rows - 1:
                    assert off == 0
                    cs_ = slice(c0, c0 + W)
                    tt(pT[pi][:rows, cs_], pT[pi][:rows, cs_], masks[:rows, 0, 0:W], ALU.mult,
                       [("pT", pi), "masks"], [("pT", pi)])

            def emit_pv(d):
                hh, i, sn, nk, ob = d["hh"], d["i"], d["sn"], d["nk"], d["ob"]
                rows, sub0 = d["rows"], d["sub0"]
                c0 = sub0 * W
                pi = sn % 4
                mm(bank(ob)[:, c0:Tq], va[:rows, i, 64 * hh:64 * hh + 128], pT[pi][:rows, c0:Tq], i == 0, i == nk - 1,
                   [("va", vsl), ("pT", pi)], [BK(ob)])

            def emit_pre(d):
                bb = biasb[d["bsl"]]
                cref = q.P // 128 + (d["q0"] + Tq) // q.BS
                ts(bb[:, 0:d["nk"]], q.cumT[:, 0:d["nk"], d["h"]], -1.0, q.carry[:, cref, d["h"]:d["h"] + 1],
                   ALU.mult, ALU.add, [("cumT", q.name), ("carry", q.name)], [("bias", d["bsl"])])

            def emit_post1(d):
                srow = 64 if d["hh"] == 0 else 0
                ops_ = bank(d["ob"])
                rsl = OB.index(d["ob"])
                s.op("dve", (lambda srow, ops_, rsl, Tq: lambda e: e.reciprocal(out=rs[rsl][srow:srow + 1, :Tq], in_=ops_[srow:srow + 1, :Tq]))(srow, ops_, rsl, Tq),
                     r=[BK(d["ob"])], w=[("rs", rsl)])

            def emit_post2(d, q=q, Tq=Tq):
                srow = 64 if d["hh"] == 0 else 0
                orow = 0 if d["hh"] == 0 else 64
                ops_ = bank(d["ob"])
                rsl = OB.index(d["ob"])
                q0 = d["q0"]
                bc = bank(6)
                mm(bc[:, :Tq], onesf[srow:srow + 1, :], rs[rsl][srow:srow + 1, :Tq], True, True, ["onesf", ("rs", rsl)], [BK(6)])
                cp_("dve", bcs[orow:orow + 64, :Tq], bc[orow:orow + 64, :Tq], [BK(6)], ["bcs"])
                tt(ybst[orow:orow + 64, q0:q0 + Tq], ops_[orow:orow + 64, :Tq], bcs[orow:orow + 64, :Tq], ALU.mult,
                   [BK(d["ob"]), "bcs"], ["ybst"])

            nqk = 0
            npair = 0
            if hh == 0:
                pending = []
            pres = [d for k_, d in ev if k_ == "pre"]
            npre = 0
            delayed = None
            emit_pre(pres[0])
            for kind, d in ev:
                if kind == "pre":
                    npre += 1
                    if npre < len(pres):
                        emit_pre(pres[npre])
                elif kind == "pair":
                    while nqk < min(len(pairs), npair + 1 + LA):
                        emit_qk(pairs[nqk])
                        nqk += 1
                    emit_pair(d)
                    if delayed is not None:
                        emit_pv(delayed)
                    delayed = d
                    npair += 1
                    if nxt_chunks and npair % 2 == 0:
                        nxt_chunks.pop(0)()
                    for pd in pending:
                        pd[0] -= 1
                    while pending and pending[0][0] <= 0:
                        pending.pop(0)[1]()
                else:
                    if delayed is not None:
                        emit_pv(delayed)
                        delayed = None
                    emit_post1(d)
                    pending.append([DEFER, (lambda d=d, f=emit_post2: f(d))])
            while nxt_chunks:
                nxt_chunks.pop(0)()
            if hh == 1:
                while pending:
                    pending.pop(0)[1]()
                dma("pool", q.yS[hp_], ybst[:, :q.L], ["ybst"], [("yS", q.name)], "ybstst")

        s.barrier()
        st["off"] = const_end
        wra = alloc([128, 8, 128], BF16); wrx = alloc([128, 8, 128], BF16)
        for wt, src, kk in ((wra, w_rg_a, "wra"), (wrx, w_rg_x, "wrx")):
            memset(wt, 0.0, [kk])
            sv = src.rearrange("(c e) j k -> e j c k", e=2)
            for e_ in range(2):
                dma("pool", wt[64 * e_:64 * e_ + 64, :, 64 * e_:64 * e_ + 64], sv[e_], [], [kk], ("c6", kk))
        wb0 = alloc([128, 8, D], BF16); wb1 = alloc([128, 8, D], BF16)
        wo = alloc([128, 8, D], BF16); wg = alloc([128, 8, D], BF16); wp = alloc([128, 2, D], BF16)
        for wi, (wt, kk) in enumerate(((wb0, "wb0"), (wb1, "wb1"), (wo, "wo"), (wg, "wg"))):
            dma("sp", wt, wR3[wi * 8:(wi + 1) * 8].rearrange("k p n -> p k n"), [("wR3", wi * 8 + kc) for kc in range(8)], [kk], ("w3", kk))
        dma("sp", wp, wR3[32:34].rearrange("k p n -> p k n"), [("wR3", 32), ("wR3", 33)], ["wp"], ("w3", "wp"))
        NX3 = 2
        x3 = [alloc([128, D], F32) for _ in range(NX3)]
        ss3 = alloc([128, 4], F32); rstd3 = alloc([128, 4], F32)
        xn3 = alloc([128, D], BF16)
        xnT3 = alloc([128, 8, 512], BF16)
        yaT = alloc([128, 8, 512], BF16); gybT = alloc([128, 8, 512], BF16); mrgT = alloc([128, 8, 512], BF16)
        NW = 8
        wring = [alloc([128, 8, 128], BF16) for _ in range(NW)]
        ybt = [alloc([128, 512], BF16) for _ in range(2)]
        xab = [alloc([128, 3 + 512 + 1], F32) for _ in range(2)]
        u2 = [alloc([128, 512], F32) for _ in range(2)]
        ubf = [alloc([128, 512], BF16) for _ in range(2)]
        Z = alloc([128, 5, 512], F32)
        ZK = lambda i: ("Z", i)
        rg, ig, a_, om, hsb = (Z[:, i, :] for i in range(5))
        hresb = [alloc([128, D], F32) for _ in range(2)]
        pleb = [alloc([128, D], F32) for _ in range(2)]
        sg = alloc([128, D], F32)
        sga = alloc([128, 512], F32); sgb = alloc([128, 512], F32)
        hbfb = [alloc([128, D], BF16) for _ in range(2)]
        hTb = [alloc([128, 8, 128], BF16) for _ in range(2)]
        pbf = alloc([128, 4, PLE], BF16); peT = alloc([128, 2, 128], BF16)
        hist = alloc([128, 8, 3], F32); hstate = alloc([128, 8], F32)
        c3 = dict(w=0, x=0, y=0)
        for _ in range(24):
            mm(bank(7)[:, :], onesb[:, 0:128], onesb[:, :], True, True, ["onesb"], [BK(7)])

        def wchunk(pi, c):
            sl = c3["w"] % NW
            c3["w"] += 1
            dma("sp", wring[sl], wS3[pi * 8 + c].rearrange("p (kc n) -> p kc n", kc=8), [("wS3", pi, c)], [("wr", sl)], ("wr", sl))
            return wring[sl], ("wr", sl)

        def nle():
            pass

        for q in seqs:
            BS, T = q.BS, q.T
            nb = T // BS
            ntile = q.L // T
            if q.c0 is None:
                memset(hist, 0.0, ["hist"], eng="dve")
                memset(hstate, 0.0, ["hstate"], eng="dve")
            else:
                for r_ in range(3):
                    dma("sp", hist[:, :, r_:r_ + 1], q.c0[r_].rearrange("(c p o) -> p c o", p=128, o=1), [], ["hist"], "hist", slow=True)
                dma("sp", hstate.rearrange("p (c o) -> p c o", o=1), q.h0.rearrange("(c p o) -> p c o", p=128, o=1), [], ["hstate"], "hstate", slow=True)

            def do_front(j, b):
                xi = c3["x"] % NX3
                c3["x"] += 1
                dma("pool", pbf[:BS, b, :], q.p[j * T + b * BS:j * T + (b + 1) * BS, :], [], [("pbf", b)], ("pbf", b))
                yield from front_gen(q, j * T + b * BS, x3[xi], ("x3", xi), ss3[:, 0:1], rstd3[:, 0:1], ("ss3", 0),
                                     xn3, "xn3", xnT3, ("xnT3", b), b, 6)

            for b in range(nb):
                run_interleaved([do_front(0, b)])
            for j in range(ntile):
                xkall = [("xnT3", b) for b in range(nb)]

                def A_pe(c):
                    wxa, kxa = wchunk(0, c)
                    bi = 0 if c % 2 == 0 else 5
                    ps = bank(bi)
                    for kc in range(8):
                        mm(ps[:, :T], wxa[:, kc, :], xnT3[:, kc, :T], kc == 0, kc == 7, xkall + [kxa], [BK(bi)])

                def A_act(c):
                    k = c % 2
                    bi = 0 if c % 2 == 0 else 5
                    cp_("dve", xab[k][:, 0:3], hist[:, c, :], ["hist"], [("xab", k)])
                    cp_("act", xab[k][:, 3:3 + T], bank(bi)[:, :T], [BK(bi)], [("xab", k)])
                    cp_("dve", hist[:, c, :], xab[k][:, T:T + 3], [("xab", k)], ["hist"])

                def A_pool(c):
                    k = c % 2
                    x_, u_ = xab[k], u2[k]
                    ts(u_[:, :T], x_[:, 3:3 + T], cw[:, c, 3:4], cb[:, c:c + 1], ALU.mult, ALU.add, [("xab", k), "cw", "cb"], [("u", k)])
                    for k_ in (1, 2, 3):
                        stt(u_[:, :T], x_[:, 3 - k_:3 - k_ + T], cw[:, c, 3 - k_:4 - k_], u_[:, :T], ALU.mult, ALU.add,
                            [("xab", k), "cw", ("u", k)], [("u", k)])
                    cp_("dve", ubf[k][:, :T], u_[:, :T], [("u", k)], [("ubf", k)])

                def C_pe(c):
                    ba_, bb_ = (3, 4) if c % 2 == 0 else (6, 7)
                    wga, kga = wchunk(1, c)
                    for kc in range(8):
                        mm(bank(ba_)[:, :T], wga[:, kc, :], xnT3[:, kc, :T], kc == 0, kc == 7, xkall + [kga], [BK(ba_)])
                    wgb, kgb = wchunk(2, c)
                    for kc in range(8):
                        mm(bank(bb_)[:, :T], wgb[:, kc, :], xnT3[:, kc, :T], kc == 0, kc == 7, xkall + [kgb], [BK(bb_)])

                def C_act(c):
                    ba_, bb_ = (3, 4) if c % 2 == 0 else (6, 7)
                    act(sga[:, :T], bank(ba_)[:, :T], AF.Tanh, [BK(ba_)], ["sga"], scale=0.5)
                    act(sgb[:, :T], bank(bb_)[:, :T], AF.Tanh, [BK(bb_)], ["sgb"], scale=0.5)
                    stt(sga[:, :T], sga[:, :T], 1.0, bank(ba_)[:, :T], ALU.add, ALU.mult, ["sga", BK(ba_)], ["sga"])
                    stt(sgb[:, :T], sgb[:, :T], 1.0, bank(bb_)[:, :T], ALU.add, ALU.mult, ["sgb", BK(bb_)], ["sgb"])

                def B_(c):
                    k = c % 2
                    u_ = u2[k]
                    mm(bank(1)[:, :T], wra[:, c, :], ubf[k][:, :T], True, True, ["wra", ("ubf", k)], [BK(1)])
                    mm(bank(2)[:, :T], wrx[:, c, :], ubf[k][:, :T], True, True, ["wrx", ("ubf", k)], [BK(2)])
                    act(rg[:, :T], bank(1)[:, :T], AF.Tanh, [BK(1), "hbra"], [ZK(0)], bias=hbra[:, c:c + 1], scale=0.5)
                    act(ig[:, :T], bank(2)[:, :T], AF.Tanh, [BK(2), "hbrx"], [ZK(1)], bias=hbrx[:, c:c + 1], scale=0.5)
                    act(a_[:, :T], rg[:, :T], AF.Exp, [ZK(0), "hnsp8"], [ZK(2)], scale=hnsp8[:, c:c + 1], bias=hnsp8[:, c:c + 1])
                    act(rg[:, :T], rg[:, :T], AF.Tanh, [ZK(0), "hpsp8"], [ZK(0)], scale=hpsp8[:, c:c + 1], bias=hpsp8[:, c:c + 1])
                    tt(om[:, :T], a_[:, :T], a_[:, :T], ALU.mult, [ZK(2)], [ZK(3)])
                    stt(om[:, :T], om[:, :T], 1.0, rg[:, :T], ALU.add, ALU.mult, [ZK(3), ZK(0)], [ZK(3)])
                    act(om[:, :T], om[:, :T], AF.Ln, [ZK(3)], [ZK(3)])
                    act(om[:, :T], om[:, :T], AF.Exp, [ZK(3)], [ZK(3)], scale=0.5)
                    stt(ig[:, :T], ig[:, :T], 1.0, u_[:, :T], ALU.add, ALU.mult, [ZK(1), ("u", k)], [ZK(1)])
                    stt(ig[:, :T], ig[:, :T], 0.5, om[:, :T], ALU.mult, ALU.mult, [ZK(1), ZK(3)], [ZK(1)])
                    s.op("dve", (lambda c, T: lambda e: e.tensor_tensor_scan(out=hsb[:, :T], data0=a_[:, :T], data1=ig[:, :T],
                                                                           initial=hstate[:, c:c + 1], op0=ALU.mult, op1=ALU.add))(c, T),
                         r=[ZK(2), ZK(1), "hstate"], w=[ZK(4)])
                    cp_("dve", hstate[:, c:c + 1], hsb[:, T - 1:T], [ZK(4)], ["hstate"])

                def C_dve(c):
                    stt(yaT[:, c, :T], sga[:, :T], 0.5, hsb[:, :T], ALU.mult, ALU.mult, [ZK(4), "sga"], ["yaT"])
                    yi = c3["y"] % 2
                    c3["y"] += 1
                    dma("sp", ybt[yi][:, :T], q.yS[c, :, j * T:(j + 1) * T], [("yS", q.name)], [("ybt", yi)], ("ybt", yi))
                    stt(gybT[:, c, :T], sgb[:, :T], 0.5, ybt[yi][:, :T], ALU.mult, ALU.mult, [("ybt", yi), "sgb"], ["gybT"])

                A_pe(0); A_act(0); A_pool(0)
                for c in range(8):
                    C_pe(c)
                    if c < 7:
                        A_pe(c + 1)
                    C_act(c)
                    if c < 7:
                        A_act(c + 1)
                        A_pool(c + 1)
                    B_(c)
                    C_dve(c)
                sma, smb = rg, ig
                for c in range(8):
                    wma, kma = wchunk(3, c)
                    wmb, kmb = wchunk(4, c)
                    for wt_, kw_, bi in ((wma, kma, 4), (wmb, kmb, 5)):
                        ps = bank(bi)
                        for kc in range(8):
                            mm(ps[:, :T], wt_[:, kc, :], xnT3[:, kc, :T], kc == 0, kc == 7, xkall + [kw_], [BK(bi)])
                    ba = 2 * (c % 2)
                    for wt_, src, kk, bi in ((wb0, yaT, "yaT", ba), (wb1, gybT, "gybT", ba + 1)):
                        ps = bank(bi)
                        for kc in range(8):
                            mm(ps[:, :T], wt_[:, kc, c * 128:(c + 1) * 128], src[:, kc, :T], kc == 0, kc == 7,
                               [kk, "wb0", "wb1"], [BK(bi)])
                    act(sma[:, :T], bank(4)[:, :T], AF.Sigmoid, [BK(4)], [ZK(0)])
                    act(smb[:, :T], bank(5)[:, :T], AF.Sigmoid, [BK(5)], [ZK(1)])
                    tt(sma[:, :T], bank(ba)[:, :T], sma[:, :T], ALU.mult, [BK(ba), ZK(0)], [ZK(0)])
                    tt(smb[:, :T], bank(ba + 1)[:, :T], smb[:, :T], ALU.mult, [BK(ba + 1), ZK(1)], [ZK(1)])
                    tt(mrgT[:, c, :T], sma[:, :T], smb[:, :T], ALU.add, [ZK(0), ZK(1)], [("mrgT", c)])
                mk = [("mrgT", c) for c in range(8)]

                def T1(b):
                    k = b % 2
                    hres, tmp, hbf, hT = hresb[k], pleb[k], hbfb[k], hTb[k]
                    tok0 = j * T + b * BS
                    dma("sp", hres[:BS], q.x[tok0:tok0 + BS, :], [], [("hres", k)], ("xres", k))
                    mix = pair(0)
                    for half in range(2):
                        for kc in range(8):
                            mm(mix[:BS, half * 512:(half + 1) * 512], mrgT[:, kc, b * BS:(b + 1) * BS],
                               wo[:, kc, half * 512:(half + 1) * 512], kc == 0, kc == 7, mk + ["wo"], [BK(0), BK(1)])
                    yield
                    act(junk[:BS], mix[:BS], AF.Square, [BK(0), BK(1)], ["junk", ("ss3", 1)], accum=ss3[:BS, 1:2])
                    yield
                    rstd_pool(rstd3[:, 1:2], ss3[:, 1:2], BS, ("ss3", 1))
                    yield
                    stt(tmp[:BS], mix[:BS], rstd3[:BS, 1:2], gpost_bc[:BS], ALU.mult, ALU.mult,
                        [BK(0), BK(1), ("ss3", 1), "gpost"], [("ple", k)])
                    tt(hres[:BS], hres[:BS], tmp[:BS], ALU.add, [("hres", k), ("ple", k)], [("hres", k)])
                    cp_("dve", hbf[:BS], hres[:BS], [("hres", k)], [("hbf", k)])
                    yield
                    tp = bankb(7)
                    for kc in range(8):
                        tr(tp[:, kc * BS:(kc + 1) * BS], hbf[:BS, kc * 128:(kc + 1) * 128], ident[:BS, :BS], [("hbf", k), "ident"], [BK(7)])
                    yield
                    cp_("act", hT[:, :, :BS], tp[:, 0:8 * BS].rearrange("p (k t) -> p k t", k=8), [BK(7)], [("hT", k)])

                def T2(b):
                    k = b % 2
                    hres, ple, hT = hresb[k], pleb[k], hTb[k]
                    tok0 = j * T + b * BS
                    gp = pair(1)
                    for half in range(2):
                        for kc in range(8):
                            mm(gp[:BS, half * 512:(half + 1) * 512], hT[:, kc, :BS], wg[:, kc, half * 512:(half + 1) * 512],
                               kc == 0, kc == 7, [("hT", k), "wg"], [BK(2), BK(3)])
                    tp2 = bankb(6)
                    for kc in range(2):
                        tr(tp2[:, kc * BS:(kc + 1) * BS], pbf[:BS, b, kc * 128:(kc + 1) * 128], ident[:BS, :BS], [("pbf", b), "ident"], [BK(6)])
                    yield
                    cp_("act", peT[:, :, :BS], tp2[:, 0:2 * BS].rearrange("p (k t) -> p k t", k=2), [BK(6)], ["peT"])
                    pq = pair(2)
                    for half in range(2):
                        for kc in range(2):
                            mm(pq[:BS, half * 512:(half + 1) * 512], peT[:, kc, :BS], wp[:, kc, half * 512:(half + 1) * 512],
                               kc == 0, kc == 1, ["peT", "wp"], [BK(4), BK(5)])
                    act(sg[:BS], gp[:BS], AF.Sigmoid, [BK(2), BK(3)], ["sg"])
                    yield
                    tt(ple[:BS], pq[:BS], sg[:BS], ALU.mult, [BK(4), BK(5), "sg"], [("ple", k)])
                    yield
                    act(junk[:BS], ple[:BS], AF.Square, [("ple", k)], ["junk", ("ss3", 2)], accum=ss3[:BS, 2:3])
                    yield
                    rstd_pool(rstd3[:, 2:3], ss3[:, 2:3], BS, ("ss3", 2))
                    yield
                    stt(ple[:BS], ple[:BS], rstd3[:BS, 2:3], gple_bc[:BS], ALU.mult, ALU.mult, [("ple", k), ("ss3", 2), "gple"], [("ple", k)])
                    tt(ple[:BS], ple[:BS], hres[:BS], ALU.add, [("ple", k), ("hres", k)], [("ple", k)])
                    dma("pool", q.y[tok0:tok0 + BS, :], ple[:BS], [("ple", k)], [], ("yout", k))

                run_interleaved([T1(0)])
                for b in range(nb):
                    run_interleaved([T1(b + 1) if b + 1 < nb else None, T2(b),
                                     do_front(j + 1, b) if j + 1 < ntile else None])
            for r_ in range(3):
                dma("pool", q.cout[r_].rearrange("(c p o) -> p c o", p=128, o=1), hist[:, :, r_:r_ + 1], ["hist"], [], "cout", slow=True)
            dma("pool", q.hout.rearrange("(c p o) -> p c o", p=128, o=1), hstate.rearrange("p (c o) -> p c o", o=1), ["hstate"], [], "hout", slow=True)
        print("ops", len(s.ops))
        s.emit()
        print("nsem", s.nsem)
    return nc


_CACHE = {}


def _get_nc(S):
    if S not in _CACHE:
        _CACHE[S] = build(S)
    return _CACHE[S]


def kernel(x_prompt, x_sample, p_prompt, p_sample, cache_k, cache_v, cache_logf, state_h, state_conv,
           w_in, conv_w, conv_b, w_rg_a, b_rg_a, w_rg_x, b_rg_x, a_param, b_f, w_branch, w_o,
           g_pre, g_post, w_ple_gate, w_ple_proj, g_ple, _ncores=8):
    f = lambda a: np.ascontiguousarray(np.asarray(a, dtype=np.float32))
    B, S, _ = x_prompt.shape
    n = _ncores
    nc = _get_nc(S)
    shared = dict(w_in=f(w_in[0]), conv_w=f(conv_w[0]), conv_b=f(conv_b[0]), w_rg_a=f(w_rg_a[0]), b_rg_a=f(b_rg_a[0]),
                  w_rg_x=f(w_rg_x[0]), b_rg_x=f(b_rg_x[0]), a_param=f(a_param[0]), b_f=f(b_f[0]),
                  w_branch=f(w_branch[0]), w_o=f(w_o[0]), g_pre=f(g_pre[0]), g_post=f(g_post[0]),
                  w_pg=f(w_ple_gate[0]), w_pp=f(w_ple_proj[0]), g_ple=f(g_ple[0]))
    in_maps = []
    for c in range(n):
        m = dict(shared)
        m["xp"] = f(x_prompt[c]); m["pp"] = f(p_prompt[0, c])
        m["xs"] = f(x_sample[2 * c:2 * c + 2]); m["psm"] = f(p_sample[0, 2 * c:2 * c + 2])
        m["ck"] = f(np.asarray(cache_k[0, 2 * c:2 * c + 2]).reshape(2, PAST, D))
        m["cv"] = f(np.asarray(cache_v[0, 2 * c:2 * c + 2]).reshape(2, PAST, D))
        m["clf"] = f(cache_logf[0, 2 * c:2 * c + 2])
        m["sh"] = f(state_h[0, 2 * c:2 * c + 2]); m["sc"] = f(state_conv[0, 2 * c:2 * c + 2])
        in_maps.append(m)
    res = run_bass_kernel_spmd(nc, in_maps, core_ids=list(range(n))).results
    cat = lambda k: np.stack([np.asarray(r[k]) for r in res], 0)
    cat2 = lambda k: np.concatenate([np.asarray(r[k]) for r in res], 0)
    y_p = cat("yp"); y_s = cat2("ys")
    k_p = cat("kp").reshape(1, n, S, H, 64); v_p = cat("vp").reshape(1, n, S, H, 64)
    lf_p = cat("lfp").reshape(1, n, S, H)
    h_p = cat("hp").reshape(1, n, D); c_p = cat("cp").reshape(1, n, 3, D)
    k_s = cat2("ks").reshape(1, 2 * n, DEC, H, 64); v_s = cat2("vs").reshape(1, 2 * n, DEC, H, 64)
    lf_s = cat2("lfs").reshape(1, 2 * n, DEC, H)
    h_s = cat2("hs").reshape(1, 2 * n, D); c_s = cat2("cs").reshape(1, 2 * n, 3, D)
    return (y_p, y_s, k_p, v_p, lf_p, h_p, c_p, k_s, v_s, lf_s, h_s, c_s)
```
